# Optimizing a Trainium2 kernel written in Bass

```python
import jax, jax.numpy as jnp
from jax import lax
import numpy as np

D_MODEL = 1024
BATCH = 2
SEQ = 16384
DEPTH = 4
DEC_BATCH = 1
DEC_SEQ = 16384
PAST_LEN = 128

GRID_W = 64
N_HEADS = 16
HEAD_DIM = D_MODEL // N_HEADS
WIN_H = 8
WIN_W = 16
Q_COL_BLOCK = 16
K_COL_BAND = 32
CONV_K = 31
FFN_CONV_K = 3
D_FF = 2816
N_MIXERS = 2
N_ATTN = (DEPTH + 1) // 2
N_CONV = DEPTH // 2
RMS_EPS = 1e-6
LN_EPS = 1e-5

kernel_name = 'hybrid_natten_conformer_encoder'


def rms_norm(x, g):
    x32 = x.astype(jnp.float32)
    y = x32 * lax.rsqrt(jnp.mean(x32 * x32, axis=-1, keepdims=True) + RMS_EPS)
    return y.astype(x.dtype) * g


def layer_norm(x, g, b):
    x32 = x.astype(jnp.float32)
    mu = jnp.mean(x32, axis=-1, keepdims=True)
    xc = x32 - mu
    var = jnp.mean(xc * xc, axis=-1, keepdims=True)
    return (xc * lax.rsqrt(var + LN_EPS)).astype(x.dtype) * g + b


def depthwise_conv(x, w, b):
    y = lax.conv_general_dilated(x, w[:, None, :], window_strides=(1,), padding='SAME',
                                 dimension_numbers=('NWC', 'WIO', 'NWC'),
                                 feature_group_count=x.shape[-1])
    return y + b


def neighbourhood_attention(x, w_qkv, b_qkv, rpb, w_o, b_o):
    bsz, seq, _ = x.shape
    rows = seq // GRID_W
    kh = min(WIN_H, rows)
    n_cb = GRID_W // Q_COL_BLOCK
    qkv = x @ w_qkv + b_qkv
    q, k, v = jnp.split(qkv, 3, axis=-1)
    q = q * (HEAD_DIM ** -0.5)
    shape5 = (bsz, rows, GRID_W, N_HEADS, HEAD_DIM)
    q, k, v = q.reshape(shape5), k.reshape(shape5), v.reshape(shape5)
    q_col = np.arange(GRID_W).reshape(n_cb, Q_COL_BLOCK)
    band_start = np.clip(q_col[:, 0] - WIN_W // 2, 0, GRID_W - K_COL_BAND)
    key_col = band_start[:, None] + np.arange(K_COL_BAND)[None, :]
    win_start = np.clip(q_col - WIN_W // 2, 0, GRID_W - WIN_W)
    kc = key_col[:, None, :]
    ws = win_start[..., None]
    col_ok = (kc >= ws) & (kc < ws + WIN_W)
    dc_idx = np.clip(kc - q_col[..., None] + WIN_W - 1, 0, 2 * WIN_W - 2)
    col_bias = rpb[:, :, dc_idx]
    mask = jnp.asarray(col_ok)[:, :, None, :]

    def one_row(r):
        rs = jnp.clip(r - kh // 2, 0, rows - kh)
        k_band = lax.dynamic_slice_in_dim(k, rs, kh, axis=1)[:, :, key_col]
        v_band = lax.dynamic_slice_in_dim(v, rs, kh, axis=1)[:, :, key_col]
        q_blk = q[:, r].reshape(bsz, n_cb, Q_COL_BLOCK, N_HEADS, HEAD_DIM)
        s = jnp.einsum('bjqhd,bkjlhd->bhjqkl', q_blk, k_band)
        dr_idx = rs + jnp.arange(kh) - r + (WIN_H - 1)
        bias = jnp.take(col_bias, dr_idx, axis=1).transpose(0, 2, 3, 1, 4)
        s = s.astype(jnp.float32) + bias.astype(jnp.float32)
        s = jnp.where(mask, s, -jnp.inf)
        p = jax.nn.softmax(s.reshape(s.shape[:4] + (kh * K_COL_BAND,)), axis=-1)
        p = p.reshape(s.shape).astype(v.dtype)
        o = jnp.einsum('bhjqkl,bkjlhd->bjqhd', p, v_band)
        return o.reshape(bsz, GRID_W, D_MODEL)

    out = lax.map(one_row, jnp.arange(rows))
    out = out.transpose(1, 0, 2, 3).reshape(bsz, seq, D_MODEL)
    return out @ w_o + b_o


def conformer_conv(x, w_pw1, b_pw1, w_dw, b_dw, ln_g, ln_b, w_pw2, b_pw2):
    h = x @ w_pw1 + b_pw1
    a, g = jnp.split(h, 2, axis=-1)
    h = a * jax.nn.sigmoid(g)
    h = depthwise_conv(h, w_dw, b_dw)
    h = jax.nn.silu(layer_norm(h, ln_g, ln_b))
    return h @ w_pw2 + b_pw2


def conv_glu_ffn(x, w_up, w_dw, b_dw, w_down):
    h = depthwise_conv(x @ w_up, w_dw, b_dw)
    g, u = jnp.split(h, 2, axis=-1)
    return (jax.nn.silu(g) * u) @ w_down


def trunk(x, attn_w_qkv, attn_b_qkv, attn_rpb, attn_w_o, attn_b_o,
          conv_w_pw1, conv_b_pw1, conv_w_dw, conv_b_dw, conv_ln_g, conv_ln_b, conv_w_pw2, conv_b_pw2,
          ffn_w_up, ffn_w_dw, ffn_b_dw, ffn_w_down, norm_mix, norm_ffn, norm_final):
    for i in range(DEPTH):
        h = rms_norm(x, norm_mix[i])
        j = i // N_MIXERS
        if i % N_MIXERS == 0:
            x = x + neighbourhood_attention(h, attn_w_qkv[j], attn_b_qkv[j], attn_rpb[j], attn_w_o[j], attn_b_o[j])
        else:
            x = x + conformer_conv(h, conv_w_pw1[j], conv_b_pw1[j], conv_w_dw[j], conv_b_dw[j],
                                   conv_ln_g[j], conv_ln_b[j], conv_w_pw2[j], conv_b_pw2[j])
        x = x + conv_glu_ffn(rms_norm(x, norm_ffn[i]), ffn_w_up[i], ffn_w_dw[i], ffn_b_dw[i], ffn_w_down[i])
    return rms_norm(x, norm_final)


def setup_inputs(seed: int = 0) -> dict:
    key = jax.random.key(seed)
    ks = jax.random.split(key, 22)
    D, F, H = D_MODEL, D_FF, N_HEADS
    res_scale = (2 * DEPTH) ** -0.5
    nrm = lambda k, shape, s: jax.random.normal(k, shape, jnp.float32) * s
    return {
        'x_prompt': nrm(ks[0], (BATCH, SEQ, D), 1.0),
        'x_sample': nrm(ks[1], (DEC_BATCH, DEC_SEQ, D), 1.0),
        'attn_w_qkv': nrm(ks[2], (N_ATTN, D, 3 * D), D ** -0.5),
        'attn_b_qkv': nrm(ks[3], (N_ATTN, 3 * D), 0.02),
        'attn_rpb': nrm(ks[4], (N_ATTN, H, 2 * WIN_H - 1, 2 * WIN_W - 1), 0.5),
        'attn_w_o': nrm(ks[5], (N_ATTN, D, D), D ** -0.5 * res_scale),
        'attn_b_o': nrm(ks[6], (N_ATTN, D), 0.02),
        'conv_w_pw1': nrm(ks[7], (N_CONV, D, 2 * D), D ** -0.5),
        'conv_b_pw1': nrm(ks[8], (N_CONV, 2 * D), 0.02),
        'conv_w_dw': nrm(ks[9], (N_CONV, CONV_K, D), CONV_K ** -0.5),
        'conv_b_dw': nrm(ks[10], (N_CONV, D), 0.02),
        'conv_ln_g': 1.0 + nrm(ks[11], (N_CONV, D), 0.05),
        'conv_ln_b': nrm(ks[12], (N_CONV, D), 0.02),
        'conv_w_pw2': nrm(ks[13], (N_CONV, D, D), D ** -0.5 * res_scale),
        'conv_b_pw2': nrm(ks[14], (N_CONV, D), 0.02),
        'ffn_w_up': nrm(ks[15], (DEPTH, D, 2 * F), D ** -0.5),
        'ffn_w_dw': nrm(ks[16], (DEPTH, FFN_CONV_K, 2 * F), FFN_CONV_K ** -0.5),
        'ffn_b_dw': nrm(ks[17], (DEPTH, 2 * F), 0.02),
        'ffn_w_down': nrm(ks[18], (DEPTH, F, D), F ** -0.5 * res_scale),
        'norm_mix': 1.0 + nrm(ks[19], (DEPTH, D), 0.05),
        'norm_ffn': 1.0 + nrm(ks[20], (DEPTH, D), 0.05),
        'norm_final': 1.0 + nrm(ks[21], (D,), 0.05),
    }


def reference(x_prompt, x_sample, attn_w_qkv, attn_b_qkv, attn_rpb, attn_w_o, attn_b_o,
              conv_w_pw1, conv_b_pw1, conv_w_dw, conv_b_dw, conv_ln_g, conv_ln_b, conv_w_pw2, conv_b_pw2,
              ffn_w_up, ffn_w_dw, ffn_b_dw, ffn_w_down, norm_mix, norm_ffn, norm_final):
    y_prompt = trunk(x_prompt, attn_w_qkv, attn_b_qkv, attn_rpb, attn_w_o, attn_b_o,
                     conv_w_pw1, conv_b_pw1, conv_w_dw, conv_b_dw, conv_ln_g, conv_ln_b, conv_w_pw2, conv_b_pw2,
                     ffn_w_up, ffn_w_dw, ffn_b_dw, ffn_w_down, norm_mix, norm_ffn, norm_final)
    y_sample = trunk(x_sample, attn_w_qkv, attn_b_qkv, attn_rpb, attn_w_o, attn_b_o,
                     conv_w_pw1, conv_b_pw1, conv_w_dw, conv_b_dw, conv_ln_g, conv_ln_b, conv_w_pw2, conv_b_pw2,
                     ffn_w_up, ffn_w_dw, ffn_b_dw, ffn_w_down, norm_mix, norm_ffn, norm_final)
    return (y_prompt, y_sample)
```

```python
import contextlib
import os
import numpy as np
import concourse.bass as bass
import concourse.mybir as mybir
from concourse.bass_utils import run_bass_kernel_spmd

F32 = mybir.dt.float32
BF16 = mybir.dt.bfloat16
AF = mybir.ActivationFunctionType
ALU = mybir.AluOpType

NCORE = 8
D = 1024
DFF = 2816
NH = 16
GW = 64
ROWS = 256
GAP = 8
CORE_ROWS = 98
HALO_B = 12
HALO_A = 10
WR = CORE_ROWS + HALO_B + HALO_A
NT = WR // 2
T = WR * GW
SEQ_OFF = [0, ROWS + GAP, 2 * (ROWS + GAP)]
GROWS = NCORE * CORE_ROWS
NEG = -30000.0
OFFS = [-3, -2, -1, 0, 1, 2, 3]
HGRP = [[2 * (4 * (g // 2) + i) + (g % 2) for i in range(4)] for g in range(4)]
FB = 510
DEPTH = 4


class Res:
    __slots__ = ("w", "r")

    def __init__(self):
        self.w = None
        self.r = []


class Ctx:
    def __init__(self, nc):
        self.nc = nc
        self.eng = {"pe": nc.tensor, "act": nc.scalar, "dve": nc.vector, "pool": nc.gpsimd, "sp": nc.sync}
        self.sems = {}
        self.cnt = {}
        self.waited = {k: {} for k in self.eng}
        self._stack = []
        self.ninst = 0
        self.keymap = {}

    def sem(self, key):
        if key not in self.sems:
            cm = self.nc.semaphore("s_" + key)
            self.sems[key] = cm.__enter__()
            self._stack.append(cm)
            self.cnt[key] = 0
        return self.sems[key]

    def _need(self, e, deps):
        w = self.waited[e]
        best = {}
        for (k, v) in deps:
            if w.get(k, 0) < v and best.get(k, 0) < v:
                best[k] = v
        for k, v in best.items():
            self.eng[e].wait_ge(self.sems[k], v)
            w[k] = v

    def _deps(self, e, reads, writes):
        deps = []
        for r in reads:
            if r.w is not None:
                deps.append(r.w)
        for x in writes:
            if x.w is not None:
                deps.append(x.w)
            deps.extend(x.r)
        if e == "pe":
            deps = [d for d in deps if d[0] != "pe"]
        return deps

    def _record(self, tok, reads, writes):
        for r in reads:
            r.r.append(tok)
            if len(r.r) > 64:
                best = {}
                for (k, v) in r.r:
                    if best.get(k, 0) < v:
                        best[k] = v
                r.r = list(best.items())
        for x in writes:
            x.w = tok
            x.r = []

    def op(self, e, fn, reads=(), writes=(), signal=True):
        self.sem(e)
        self._need(e, self._deps(e, reads, writes))
        inst = fn(self.eng[e])
        self.ninst += 1
        if signal:
            self.cnt[e] += 1
            inst.then_inc(self.sems[e], 1)
            tok = (e, self.cnt[e])
        else:
            tok = (e, self.cnt[e] + 1)
        self._record(tok, reads, writes)
        return inst

    def dma(self, q, semkey, out, in_, reads=(), writes=(), **kw):
        km = self.keymap.setdefault(q, {})
        if semkey not in km:
            km[semkey] = "%s_g%d" % (q, len(km))
        semkey = km[semkey]
        self.sem(semkey)
        self._need(q, self._deps(q, reads, writes))
        inst = self.eng[q].dma_start(out=out, in_=in_, **kw)
        self.ninst += 1
        self.cnt[semkey] += 16
        inst.then_inc(self.sems[semkey], 16)
        self._record((semkey, self.cnt[semkey]), reads, writes)
        return inst

    def barrier(self):
        allt = [(k, v) for k, v in self.cnt.items() if v > 0]
        for e in self.eng:
            self._need(e, allt)
        self.keymap = {}

    def close(self):
        for cm in reversed(self._stack):
            cm.__exit__(None, None, None)


class Phase:
    def __init__(self, nc, name, side=None):
        self.nc = nc
        self.name = name
        self.side = side
        self.es = contextlib.ExitStack()
        self.n = 0

    def __enter__(self):
        self.es.__enter__()
        return self

    def __exit__(self, *a):
        return self.es.__exit__(*a)

    def sb(self, shape, dt):
        self.n += 1
        return self.es.enter_context(self.nc.sbuf_tensor("%s_s%d" % (self.name, self.n), list(shape), dt, side=self.side))

    def ps(self, shape, dt):
        self.n += 1
        return self.es.enter_context(self.nc.psum_tensor("%s_p%d" % (self.name, self.n), list(shape), dt))

    def sbs(self, n, shape, dt):
        return [(self.sb(shape, dt), Res()) for _ in range(n)]

    def pss(self, n, shape, dt):
        return [(self.ps(shape, dt), Res()) for _ in range(n)]


def build_program(wr=WR, out_tile0=HALO_B // 2, n_final_tiles=CORE_ROWS // 2, nphases=None, tile_offs=None, dbg_rng=None, merge_q=None):
    nc = bass.Bass("TRN2", target_bir_lowering=False)
    T = wr * GW
    NT = wr // 2

    def din(name, shape, dt=F32):
        return nc.dram_tensor(name, list(shape), dt, kind="ExternalInput").ap()

    def dscr(name, shape, dt):
        return nc.dram_tensor(name, list(shape), dt, kind="Internal").ap()

    xw = din("xw", [T, D])
    tokmask_d = din("tokmask", [128, NT])
    padneg_d = din("padneg", [1, T])
    rowbias_d = din("rowbias", [128, NT * 7 * 2])
    ident_d = din("ident", [128, 128])
    w_qkv = din("w_qkv", [2, D, 3 * D])
    bqk_d = din("bqk", [2, 128, 16])
    bv_d = din("bv_bc", [2, 128, D])
    tab_d = din("tab", [2, 128, 7 * NH * 128])
    w_o = din("w_o", [2, D, D])
    bo_d = din("bo_bc", [2, 128, D])
    w_pw1 = din("w_pw1", [2, D, 2 * D])
    bpw1_d = din("bpw1", [2, 128, 16])
    wdw_d = din("wdw", [2, 128, 8 * 31])
    bdw_d = din("bdw", [2, 128, 8])
    lng_d = din("lng", [2, 128, 8])
    lnb_d = din("lnb", [2, 128, 8])
    w_pw2 = din("w_pw2", [2, D, D])
    bpw2_d = din("bpw2_bc", [2, 128, D])
    w_up = din("w_up", [4, D, 2 * DFF])
    fdw_d = din("fdw", [4, 128, 44 * 3])
    fdb_d = din("fdb", [4, 128, 44])
    w_down = din("w_down", [4, DFF, D])
    gmix_d = din("gmix_bc", [4, 128, D])
    gffn_d = din("gffn_bc", [4, 128, D])
    gfin_d = din("gfin_bc", [128, D])
    y_out = nc.dram_tensor("y", [(n_final_tiles if nphases is None else NT) * 128, D], F32, kind="ExternalOutput").ap()

    xa = dscr("xa", [T, D], F32)
    xb = dscr("xb", [T, D], F32)
    hT_d = dscr("hT", [D, T], BF16)
    qT_d = dscr("qT", [D, T], BF16)
    kT_d = dscr("kT", [D, T], BF16)
    v_d = dscr("vv", [T, NH * 65], BF16)
    gl_d = dscr("glu", [D, T], BF16)

    c = Ctx(nc)
    glob = contextlib.ExitStack()
    with glob:
        def gsb(name, shape, dt):
            return glob.enter_context(nc.sbuf_tensor(name, list(shape), dt))

        ident = gsb("ident_bf", [128, 128], BF16)
        r_ident = Res()
        tokmask = gsb("tokmask_sb", [128, NT], F32)
        r_tokmask = Res()
        mhalf = gsb("mhalf", [128, 512], F32)
        r_mhalf = Res()
        c.dma("pool", "g_ld", ident[:], ident_d[:, :], writes=[r_ident])
        c.dma("sp", "g_ld2", tokmask[:], tokmask_d[:, :], writes=[r_tokmask])
        c.op("pool", lambda e: e.memset(mhalf[:], -0.5), writes=[r_mhalf])
        c.barrier()

        def run_pipe(n, stages):
            maxlag = max(l for _, l in stages)
            for step in range(n + maxlag):
                for fn, lag in stages:
                    t = step - lag
                    if 0 <= t < n:
                        fn(t)

        def phase_norm(ph, x_src, g_dram, use_mask, tag, t_lo=0, t_hi=None):
            t_hi = NT if t_hi is None else t_hi
            if True:
                gbc = ph.sb([128, D], F32)
                r_g = Res()
                c.dma("sp", "w_a", gbc[:], g_dram, writes=[r_g])
                NS = 8
                xs = ph.sbs(NS, [128, D], F32)
                junk = ph.sbs(2, [128, D], BF16)
                ss = ph.sbs(NS, [128, 1], F32)
                rs = ph.sbs(NS, [128, 1], F32)
                hs = ph.sbs(3, [128, D], BF16)
                hts = ph.sbs(2, [128, 8, 512], BF16)
                pT = ph.pss(4, [128, D], BF16)
                hview = hT_d.rearrange("(k p) t -> p k t", p=128)

                def st_l(i):
                    t = t_lo + i
                    x_, rx = xs[i % NS]
                    c.dma("sp", "n_x%d" % (i % NS), x_[:], x_src[t * 128:(t + 1) * 128, :], writes=[rx])

                def st_a(i):
                    t = t_lo + i
                    x_, rx = xs[i % NS]
                    j_, rj = junk[i % 2]
                    s_, rss = ss[i % NS]
                    q_, rrs = rs[i % NS]
                    c.op("pool", lambda e: e.memset(s_[:], 0.0), writes=[rss])
                    c.op("act", lambda e: e.activation(out=j_[:], in_=x_[:], func=AF.Square, scale=1.0 / 32, accum_out=s_[:]),
                         reads=[rx], writes=[rj, rss])
                    c.op("dve", lambda e: e.tensor_scalar(out=s_[:], in0=s_[:], scalar1=1e-6, scalar2=None, op0=ALU.add),
                         reads=[rss], writes=[rss])
                    c.op("pool", lambda e: e.tensor_tensor(out=q_[:], in0=s_[:], in1=mhalf[:, 0:1], op=ALU.pow),
                         reads=[rss, r_mhalf], writes=[rrs])
                    if use_mask:
                        c.op("pool", lambda e: e.tensor_tensor(out=q_[:], in0=q_[:], in1=tokmask[:, t:t + 1], op=ALU.mult),
                             reads=[rrs, r_tokmask], writes=[rrs])

                def st_b(i):
                    t = t_lo + i
                    x_, rx = xs[i % NS]
                    q_, rrs = rs[i % NS]
                    h_, rh = hs[i % 3]
                    c.op("dve", lambda e: e.scalar_tensor_tensor(out=h_[:], in0=x_[:], scalar=q_[:], in1=gbc[:], op0=ALU.mult, op1=ALU.mult),
                         reads=[rx, rrs, r_g], writes=[rh])
                    p_, rp = pT[i % 4]
                    for k in range(8):
                        c.op("pe", lambda e: e.transpose(out=p_[:, k * 128:(k + 1) * 128], in_=h_[:, k * 128:(k + 1) * 128], identity=ident[:]),
                             reads=[rh, r_ident], writes=[rp], signal=(k == 7))

                def st_c(i):
                    t = t_lo + i
                    p_, rp = pT[i % 4]
                    ht_, rht = hts[(i // 4) % 2]
                    c.op("dve", lambda e: e.tensor_copy(out=ht_[:, :, (i % 4) * 128:(i % 4 + 1) * 128], in_=p_[:].rearrange("p (k t) -> p k t", k=8)),
                         reads=[rp], writes=[rht])
                    if i % 4 == 3 or t == t_hi - 1:
                        g0 = (t_lo + (i // 4) * 4) * 128
                        n = (i % 4 + 1) * 128
                        c.dma("sp", "n_st%d" % ((i // 4) % 2), hview[:, :, g0:g0 + n], ht_[:, :, 0:n], reads=[rht])

                run_pipe(t_hi - t_lo, [(st_l, 0), (st_a, 3), (st_b, 5), (st_c, 6)])
                c.barrier()

        def make_normer(ph, x_src, g_dram, use_mask):
            gbc = ph.sb([128, D], F32)
            r_g = Res()
            c.dma("sp", "nw_g", gbc[:], g_dram, writes=[r_g])
            NS = 8
            xs = ph.sbs(NS, [128, D], F32)
            junk = ph.sbs(2, [128, D], BF16)
            ss = ph.sbs(NS, [128, 1], F32)
            rs = ph.sbs(NS, [128, 1], F32)
            hs = ph.sbs(NS, [128, D], BF16)
            pT = ph.pss(2, [128, D], BF16)
            cnt = {"f": 0, "b": 0}

            def fload(tok0, nt):
                base = cnt["f"]
                cnt["f"] += nt
                for tl in range(nt):
                    i = base + tl
                    x_, rx = xs[i % NS]
                    c.dma("sp", "nw_x%d" % (i % NS), x_[:], x_src[tok0 + tl * 128:tok0 + (tl + 1) * 128, :], writes=[rx])

            cnt["c"] = 0

            def fcomp(tok0, nt):
                base = cnt["c"]
                cnt["c"] += nt
                for tl in range(nt):
                    i = base + tl
                    t = (tok0 // 128) + tl
                    x_, rx = xs[i % NS]
                    j_, rj = junk[i % 2]
                    s_, rss = ss[i % NS]
                    q_, rrs = rs[i % NS]
                    c.op("pool", lambda e: e.memset(s_[:], 0.0), writes=[rss])
                    c.op("act", lambda e: e.activation(out=j_[:], in_=x_[:], func=AF.Square, scale=1.0 / 32, accum_out=s_[:]),
                         reads=[rx], writes=[rj, rss])
                    c.op("dve", lambda e: e.tensor_scalar(out=s_[:], in0=s_[:], scalar1=1e-6, scalar2=None, op0=ALU.add),
                         reads=[rss], writes=[rss])
                    c.op("pool", lambda e: e.tensor_tensor(out=q_[:], in0=s_[:], in1=mhalf[:, 0:1], op=ALU.pow),
                         reads=[rss, r_mhalf], writes=[rrs])
                    if use_mask:
                        c.op("pool", lambda e: e.tensor_tensor(out=q_[:], in0=q_[:], in1=tokmask[:, t:t + 1], op=ALU.mult),
                             reads=[rrs, r_tokmask], writes=[rrs])
                for tl in range(nt):
                    i = base + tl
                    x_, rx = xs[i % NS]
                    q_, rrs = rs[i % NS]
                    h_, rh = hs[i % NS]
                    c.op("dve", lambda e: e.scalar_tensor_tensor(out=h_[:], in0=x_[:], scalar=q_[:], in1=gbc[:], op0=ALU.mult, op1=ALU.mult),
                         reads=[rx, rrs, r_g], writes=[rh])

            def back(nt, dst, rdst):
                base = cnt["b"]
                cnt["b"] += nt
                for tl in range(nt):
                    i = base + tl
                    h_, rh = hs[i % NS]
                    p_, rp = pT[i % 2]
                    for k in range(8):
                        c.op("pe", lambda e: e.transpose(out=p_[:, k * 128:(k + 1) * 128], in_=h_[:, k * 128:(k + 1) * 128], identity=ident[:]),
                             reads=[rh, r_ident], writes=[rp], signal=(k == 7))
                    c.op("dve", lambda e: e.tensor_copy(out=dst[:, :, tl * 128:(tl + 1) * 128], in_=p_[:].rearrange("p (k t) -> p k t", k=8)),
                         reads=[rp], writes=[rdst])

            return fload, fcomp, back

        def prep_qkv(ph, j):
            if True:
                W = ph.sb([128, 8, 3 * D], BF16)
                wv = w_qkv[j].rearrange("(k p) n -> p k n", p=128)
                rW = {}
                for cb in range(6):
                    r_ = Res()
                    c.dma("pool", "wq%d" % cb, W[:, :, cb * 512:(cb + 1) * 512], wv[:, :, cb * 512:(cb + 1) * 512], writes=[r_])
                    rW[cb] = r_
                bqk = ph.sb([128, 16], F32)
                rb = Res()
                c.dma("sp", "w_b", bqk[:], bqk_d[j], writes=[rb])
                c.op("dve", lambda e: e.tensor_scalar(out=bqk[:, 0:8], in0=bqk[:, 0:8], scalar1=0.125, scalar2=None, op0=ALU.mult),
                     reads=[rb], writes=[rb])
                bv = ph.sb([128, D], F32)
                rbv = Res()
                c.dma("sp", "w_c", bv[:], bv_d[j], writes=[rbv])
                return {'W': W, 'rW': rW, 'bqk': bqk, 'rb': rb, 'bv': bv, 'rbv': rbv}

        def run_qkv(ph, w_, j, x_src, g_dram):
            if True:
                nload, ncomp, nback = make_normer(ph, x_src, g_dram, False)
                W, rW, bqk, rb, bv, rbv = (w_['W'], w_['rW'], w_['bqk'], w_['rb'], w_['bv'], w_['rbv'])
                hb = ph.sbs(2, [128, 8, 512], BF16)
                qst = ph.sbs(2, [128, 8, 512], BF16)
                vt = ph.sbs(3, [128, NH, 65], BF16)
                for (v_, rv) in vt:
                    c.op("pool", lambda e: e.memset(v_[:], 1.0), writes=[rv])
                pq = ph.pss(3, [128, 512], F32)
                pv = ph.pss(3, [128, 512], F32)
                hview = hT_d.rearrange("(k p) t -> p k t", p=128)
                qview = qT_d.rearrange("(k p) t -> p k t", p=128)
                kview = kT_d.rearrange("(k p) t -> p k t", p=128)
                npq = 0
                npv = 0
                nv = 0
                nload(0, 4)
                ncomp(0, 4)
                nback(4, hb[0][0], hb[0][1])
                for s in range(T // 512):
                    h_, rh = hb[s % 2]
                    if s + 1 < T // 512:
                        nload((s + 1) * 512, 4)
                    for which in range(2):
                        st_, rst = qst[which]
                        if which == 1 and s + 1 < T // 512:
                            ncomp((s + 1) * 512, 4)
                        for cc in range(8):
                            col = which * D + cc * 128
                            p_, rp = pq[npq % 3]
                            npq += 1
                            for k in range(8):
                                c.op("pe", lambda e: e.matmul(p_[:], lhsT=W[:, k, col:col + 128], rhs=h_[:, k, :], start=(k == 0), stop=(k == 7)),
                                     reads=[rh, rW[col // 512]], writes=[rp], signal=(k == 7))
                            bcol = which * 8 + cc
                            c.op("act", lambda e: e.activation(out=st_[:, cc, :], in_=p_[:], func=AF.Identity,
                                                               scale=(0.125 if which == 0 else 1.0), bias=bqk[:, bcol:bcol + 1]),
                                 reads=[rp, rb], writes=[rst])
                        dst = qview if which == 0 else kview
                        c.dma("sp", "q_st%d" % which, dst[:, :, s * 512:(s + 1) * 512], st_[:], reads=[rst])
                    if s + 1 < T // 512:
                        nback(4, hb[(s + 1) % 2][0], hb[(s + 1) % 2][1])
                    for tl in range(4):
                        v_, rv = vt[nv % 3]
                        nv += 1
                        for half in range(2):
                            p_, rp = pv[npv % 3]
                            npv += 1
                            for k in range(8):
                                c.op("pe", lambda e: e.matmul(p_[:], lhsT=h_[:, k, tl * 128:(tl + 1) * 128],
                                                              rhs=W[:, k, 2 * D + half * 512:2 * D + (half + 1) * 512], start=(k == 0), stop=(k == 7)),
                                     reads=[rh, rW[4 + half]], writes=[rp], signal=(k == 7))
                            c.op("dve", lambda e: e.tensor_tensor(out=v_[:, half * 8:(half + 1) * 8, 0:64],
                                                                  in0=p_[:].rearrange("p (h d) -> p h d", d=64),
                                                                  in1=bv[:, half * 512:(half + 1) * 512].rearrange("p (h d) -> p h d", d=64), op=ALU.add),
                                 reads=[rp, rbv], writes=[rv])
                        tok0 = s * 512 + tl * 128
                        c.dma("sp", "q_sv%d" % ((nv - 1) % 3), v_d[tok0:tok0 + 128, :], v_[:].rearrange("p h d -> p (h d)"), reads=[rv])
                c.barrier()

        def prep_attn(ph, j):
            if True:
                Wo = ph.sb([128, 8, D], BF16)
                rW = Res()
                c.dma("pool", "w_a", Wo[:], w_o[j].rearrange("(k p) n -> p k n", p=128), writes=[rW])
                tab = ph.sb([128, 7, NH, 128], F32)
                rtab = Res()
                tv = tab_d[j].rearrange("p (o h q) -> p o h q", o=7, h=NH)
                for o in range(7):
                    c.dma("sp", "w_b", tab[:, o, :, :], tv[:, o, :, :], writes=[rtab])
                rowb = ph.sb([128, NT * 14], F32)
                rrow = Res()
                c.dma("sp", "w_c", rowb[:], rowbias_d[:, :], writes=[rrow])
                bo = ph.sb([1, D], BF16)
                rbo = Res()
                c.dma("pool", "w_d", bo[:], bo_d[j][0:1, :], writes=[rbo])
                ones1 = ph.sb([1, 128], BF16)
                ro1 = Res()
                c.op("pool", lambda e: e.memset(ones1[:], 1.0), writes=[ro1])
                return {'Wo': Wo, 'rW': rW, 'tab': tab, 'rtab': rtab, 'rowb': rowb, 'rrow': rrow, 'bo': bo, 'rbo': rbo, 'ones1': ones1, 'ro1': ro1}

        def run_attn(ph, w_, j, x_src, x_dst, gn_dram, q_lo=0, q_hi=None):
            q_hi = NT if q_hi is None else q_hi
            if True:
                Wo, rW, tab, rtab, rowb, rrow, bo, rbo, ones1, ro1 = (w_['Wo'], w_['rW'], w_['tab'], w_['rtab'], w_['rowb'], w_['rrow'], w_['bo'], w_['rbo'], w_['ones1'], w_['ro1'])
                qs = ph.sbs(3, [128, 8, 128], BF16)
                NR = 10
                kr = ph.sbs(NR, [128, 8, 128], BF16)
                vr = ph.sbs(NR, [128, NH * 65], BF16)
                xs = ph.sbs(4, [128, D], F32)
                xo = ph.sbs(3, [128, D], F32)
                gbc = ph.sb([128, D], F32)
                r_g = Res()
                c.dma("sp", "w_g", gbc[:], gn_dram, writes=[r_g])
                nss = ph.sbs(3, [128, 1], F32)
                nrs = ph.sbs(3, [128, 1], F32)
                nh = ph.sbs(2, [128, D], BF16)
                nhts = ph.sbs(1, [128, 8, 512], BF16)
                njunk = ph.sbs(1, [128, D], BF16)
                hview = hT_d.rearrange("(k p) t -> p k t", p=128)
                sf = ph.sbs(3, [128, 512], F32)
                pt = ph.sbs(2, [128, 7, 512], BF16)
                osb = ph.sbs(2, [128, D], BF16)
                ot = ph.sbs(2, [128, 8, 128], BF16)
                rsum = ph.sbs(2, [128, 4], F32)
                pS = ph.pss(3, [128, 512], F32)
                pO = ph.pss(2, [128, 512], F32)
                pT = ph.pss(1, [128, D], BF16)
                pY = ph.pss(1, [128, D], F32)
                qview = qT_d.rearrange("(k p) t -> p k t", p=128)
                kview = kT_d.rearrange("(k p) t -> p k t", p=128)

                def load_kv(jc):
                    k_, rk = kr[jc % NR]
                    v_, rv = vr[jc % NR]
                    c.dma("sp", "a_k%d" % (jc % NR), k_[:], kview[:, :, jc * 128:(jc + 1) * 128], writes=[rk])
                    c.dma("sp", "a_v%d" % (jc % NR), v_[:], v_d[jc * 128:(jc + 1) * 128, :], writes=[rv])

                for jc in range(max(0, q_lo - 3), min(NT, q_lo + 3)):
                    load_kv(jc)
                cnt = {"S": 0, "Sf": 0}
                units = [(m, hg) for m in range(q_lo, q_hi) for hg in range(4)]

                def valid_of(m):
                    return [(oi, m + o) for oi, o in enumerate(OFFS)
                            if 0 <= m + o < NT and (tile_offs is None or oi in tile_offs[m])]

                def load_tile(m):
                    if m + 3 < NT:
                        load_kv(m + 3)
                    q_, rq = qs[m % 3]
                    c.dma("sp", "a_q%d" % (m % 3), q_[:], qview[:, :, m * 128:(m + 1) * 128], writes=[rq])
                    x_, rx = xs[m % 4]
                    c.dma("sp", "a_x%d" % (m % 4), x_[:], x_src[m * 128:(m + 1) * 128, :], writes=[rx])

                load_tile(q_lo)

                def st_s(u):
                    m, hg = units[u]
                    if hg == 0 and m + 1 < q_hi:
                        load_tile(m + 1)
                    q_, rq = qs[m % 3]
                    pt_, rpt = pt[u % 2]
                    for (oi, jc) in valid_of(m):
                        k_, rk = kr[jc % NR]
                        ps_, rps = pS[cnt["S"] % 3]
                        cnt["S"] += 1
                        for hh in range(4):
                            h = HGRP[hg][hh]
                            cc = h // 2
                            pb = (h % 2) * 64
                            c.op("pe", lambda e: e.matmul(ps_[:, hh * 128:(hh + 1) * 128], lhsT=k_[pb:pb + 64, cc, :], rhs=q_[pb:pb + 64, cc, :],
                                                          start=True, stop=True),
                                 reads=[rk, rq], writes=[rps], signal=(hh == 3))
                        sf_, rsf = sf[cnt["Sf"] % 3]
                        cnt["Sf"] += 1
                        if merge_q is not None and (m, oi) in merge_q:
                            rcol = (m * 7 + oi) * 2
                            c.op("dve", lambda e: e.scalar_tensor_tensor(
                                out=sf_[:].rearrange("p (h q) -> p h q", h=4),
                                in0=ps_[:].rearrange("p (h q) -> p h q", h=4),
                                scalar=rowb[:, rcol:rcol + 1],
                                in1=tab[:, oi, hg * 4:(hg + 1) * 4, :],
                                op0=ALU.add, op1=ALU.add),
                                 reads=[rps, rrow, rtab], writes=[rsf])
                        for qrl in ([] if (merge_q is not None and (m, oi) in merge_q) else range(2)):
                            rcol = (m * 7 + oi) * 2 + qrl
                            c.op("dve", lambda e: e.scalar_tensor_tensor(
                                out=sf_[:].rearrange("p (h q) -> p h q", h=4)[:, :, qrl * 64:(qrl + 1) * 64],
                                in0=ps_[:].rearrange("p (h q) -> p h q", h=4)[:, :, qrl * 64:(qrl + 1) * 64],
                                scalar=rowb[:, rcol:rcol + 1],
                                in1=tab[:, oi, hg * 4:(hg + 1) * 4, qrl * 64:(qrl + 1) * 64],
                                op0=ALU.add, op1=ALU.add),
                                 reads=[rps, rrow, rtab], writes=[rsf])
                        c.op("act", lambda e: e.activation(out=pt_[:, oi, :], in_=sf_[:], func=AF.Exp), reads=[rsf], writes=[rpt])

                def st_pv(u):
                    m, hg = units[u]
                    pt_, rpt = pt[u % 2]
                    po_, rpo = pO[u % 2]
                    valid = valid_of(m)
                    for hh in range(4):
                        h = HGRP[hg][hh]
                        for vi, (oi, jc) in enumerate(valid):
                            v_, rv = vr[jc % NR]
                            c.op("pe", lambda e: e.matmul(po_[:, hh * 65:(hh + 1) * 65], lhsT=pt_[:, oi, hh * 128:(hh + 1) * 128],
                                                          rhs=v_[:, h * 65:(h + 1) * 65], start=(vi == 0), stop=(vi == len(valid) - 1)),
                                 reads=[rpt, rv], writes=[rpo], signal=(hh == 3 and vi == len(valid) - 1))

                def st_nrm(u):
                    m, hg = units[u]
                    po_, rpo = pO[u % 2]
                    o_, ro = osb[m % 2]
                    rs_, rrs = rsum[u % 2]
                    pov = po_[:, 0:260].rearrange("p (h d) -> p h d", d=65)
                    c.op("dve", lambda e: e.tensor_scalar(out=rs_[:], in0=pov[:, :, 64], scalar1=1e-20, scalar2=None, op0=ALU.max),
                         reads=[rpo], writes=[rrs])
                    c.op("dve", lambda e: e.reciprocal(rs_[:], rs_[:]), reads=[rrs], writes=[rrs])
                    for hh in range(4):
                        h = HGRP[hg][hh]
                        c.op("act", lambda e: e.activation(out=o_[:, h * 64:(h + 1) * 64], in_=po_[:, hh * 65:hh * 65 + 64], func=AF.Identity,
                                                           scale=rs_[:, hh:hh + 1]),
                             reads=[rpo, rrs], writes=[ro])

                def st_tr(u):
                    m, hg = units[u]
                    if hg != 3:
                        return
                    o_, ro = osb[m % 2]
                    p_, rp = pT[0]
                    for k in range(8):
                        c.op("pe", lambda e: e.transpose(out=p_[:, k * 128:(k + 1) * 128], in_=o_[:, k * 128:(k + 1) * 128], identity=ident[:]),
                             reads=[ro, r_ident], writes=[rp], signal=(k == 7))

                def st_ev(u):
                    m, hg = units[u]
                    if hg != 3:
                        return
                    p_, rp = pT[0]
                    ot_, rot = ot[m % 2]
                    c.op("act", lambda e: e.activation(out=ot_[:].rearrange("p k t -> p (k t)"), in_=p_[:], func=AF.Copy), reads=[rp], writes=[rot])

                def st_wo(u):
                    m, hg = units[u]
                    if hg != 3:
                        return
                    ot_, rot = ot[m % 2]
                    py_, rpy = pY[0]
                    for half in range(2):
                        for k in range(8):
                            c.op("pe", lambda e: e.matmul(py_[:, half * 512:(half + 1) * 512], lhsT=ot_[:, k, :], rhs=Wo[:, k, half * 512:(half + 1) * 512],
                                                          start=(k == 0), stop=False),
                                 reads=[rot, rW], writes=[rpy], signal=False)
                        c.op("pe", lambda e: e.matmul(py_[:, half * 512:(half + 1) * 512], lhsT=ones1[:, :], rhs=bo[:, half * 512:(half + 1) * 512],
                                                      start=False, stop=True),
                             reads=[ro1, rbo], writes=[rpy], signal=(half == 1))

                def st_res(u):
                    m, hg = units[u]
                    if hg != 3:
                        return
                    x_, rx = xs[m % 4]
                    py_, rpy = pY[0]
                    xo_, rxo = xo[m % 3]
                    c.op("dve", lambda e: e.tensor_tensor(out=xo_[:], in0=py_[:], in1=x_[:], op=ALU.add), reads=[rpy, rx], writes=[rxo])
                    c.dma("pool", "a_st%d" % (m % 3), x_dst[m * 128:(m + 1) * 128, :], xo_[:], reads=[rxo])

                def st_n1(u):
                    m, hg = units[u]
                    if hg != 3:
                        return
                    xo_, rxo = xo[m % 3]
                    s_, rss = nss[m % 3]
                    q_, rrs = nrs[m % 3]
                    j_, rj = njunk[0]
                    c.op("pool", lambda e: e.memset(s_[:], 0.0), writes=[rss])
                    c.op("act", lambda e: e.activation(out=j_[:], in_=xo_[:], func=AF.Square, scale=1.0 / 32, accum_out=s_[:]),
                         reads=[rxo], writes=[rj, rss])
                    c.op("dve", lambda e: e.tensor_scalar(out=s_[:], in0=s_[:], scalar1=1e-6, scalar2=None, op0=ALU.add), reads=[rss], writes=[rss])
                    c.op("pool", lambda e: e.tensor_tensor(out=q_[:], in0=s_[:], in1=mhalf[:, 0:1], op=ALU.pow), reads=[rss, r_mhalf], writes=[rrs])
                    c.op("pool", lambda e: e.tensor_tensor(out=q_[:], in0=q_[:], in1=tokmask[:, m:m + 1], op=ALU.mult),
                         reads=[rrs, r_tokmask], writes=[rrs])

                def st_n2(u):
                    m, hg = units[u]
                    if hg != 3:
                        return
                    xo_, rxo = xo[m % 3]
                    q_, rrs = nrs[m % 3]
                    h_, rh = nh[m % 2]
                    c.op("dve", lambda e: e.scalar_tensor_tensor(out=h_[:], in0=xo_[:], scalar=q_[:], in1=gbc[:], op0=ALU.mult, op1=ALU.mult),
                         reads=[rxo, rrs, r_g], writes=[rh])

                def st_n3(u):
                    m, hg = units[u]
                    if hg != 3:
                        return
                    h_, rh = nh[m % 2]
                    p_, rp = pT[0]
                    for k in range(8):
                        c.op("pe", lambda e: e.transpose(out=p_[:, k * 128:(k + 1) * 128], in_=h_[:, k * 128:(k + 1) * 128], identity=ident[:]),
                             reads=[rh, r_ident], writes=[rp], signal=(k == 7))

                def st_n4(u):
                    m, hg = units[u]
                    if hg != 3:
                        return
                    p_, rp = pT[0]
                    ht_, rht = nhts[0]
                    i = m - q_lo
                    c.op("dve", lambda e: e.tensor_copy(out=ht_[:, :, (i % 4) * 128:(i % 4 + 1) * 128], in_=p_[:].rearrange("p (k t) -> p k t", k=8)),
                         reads=[rp], writes=[rht])
                    if i % 4 == 3 or m == q_hi - 1:
                        g0 = (q_lo + (i // 4) * 4) * 128
                        n = (i % 4 + 1) * 128
                        c.dma("sp", "a_hst", hview[:, :, g0:g0 + n], ht_[:, :, 0:n], reads=[rht])

                run_pipe(len(units), [(st_s, 0), (st_pv, 1), (st_nrm, 2), (st_tr, 3), (st_ev, 4), (st_wo, 5), (st_res, 6),
                                      (st_n1, 7), (st_n2, 8), (st_n3, 9), (st_n4, 10)])
                c.barrier()

        def prep_glu(ph, j):
            if True:
                W = ph.sb([128, 8, 2 * D], BF16)
                wv = w_pw1[j].rearrange("(k p) n -> p k n", p=128)
                rW = {}
                for cb in range(4):
                    for hf in range(2):
                        r_ = Res()
                        lo_ = hf * D + cb * 256
                        c.dma("pool", "wg%d_%d" % (cb, hf), W[:, :, lo_:lo_ + 256], wv[:, :, lo_:lo_ + 256], writes=[r_])
                        rW[(hf, cb)] = r_
                b1 = ph.sb([128, 16], F32)
                rb = Res()
                c.dma("sp", "w_b", b1[:], bpw1_d[j], writes=[rb])
                pneg = ph.sb([1, T], BF16)
                rpn = Res()
                c.dma("pool", "w_c", pneg[:], padneg_d[:, :], writes=[rpn])
                ones1 = ph.sb([1, 128], BF16)
                ro1 = Res()
                c.op("pool", lambda e: e.memset(ones1[:], 1.0), writes=[ro1])
                return {'W': W, 'rW': rW, 'b1': b1, 'rb': rb, 'pneg': pneg, 'rpn': rpn, 'ones1': ones1, 'ro1': ro1}

        def run_glu(ph, w_, j, x_src, g_dram, t_lo=0, t_hi=None):
            t_hi = NT if t_hi is None else t_hi
            if True:
                nload, ncomp, nback = make_normer(ph, x_src, g_dram, False)
                W, rW, b1, rb, pneg, rpn, ones1, ro1 = (w_['W'], w_['rW'], w_['b1'], w_['rb'], w_['pneg'], w_['rpn'], w_['ones1'], w_['ro1'])
                hb = ph.sbs(2, [128, 8, 512], BF16)
                sg = ph.sbs(2, [128, 512], F32)
                gst = ph.sbs(2, [128, 8, 512], BF16)
                pa = ph.pss(3, [128, 512], F32)
                pg = ph.pss(3, [128, 512], F32)
                hview = hT_d.rearrange("(k p) t -> p k t", p=128)
                gview = gl_d.rearrange("(k p) t -> p k t", p=128)
                n = 0
                tk0 = t_lo * 128
                ntok = (t_hi - t_lo) * 128
                nblk = (ntok + 511) // 512

                def blkn(s):
                    return min(512, tk0 + ntok - (tk0 + s * 512))

                nload(tk0, blkn(0) // 128)
                ncomp(tk0, blkn(0) // 128)
                nback(blkn(0) // 128, hb[0][0], hb[0][1])
                for s in range(nblk):
                    h_, rh = hb[s % 2]
                    s0 = tk0 + s * 512
                    N = blkn(s)
                    if s + 1 < nblk:
                        nload(tk0 + (s + 1) * 512, blkn(s + 1) // 128)
                    st_, rst = gst[s % 2]
                    for cc in range(8):
                        pa_, rpa = pa[n % 3]
                        pg_, rpg = pg[n % 3]
                        sg_, rsg = sg[n % 2]
                        n += 1
                        if cc == 3 and s + 1 < nblk:
                            ncomp(tk0 + (s + 1) * 512, blkn(s + 1) // 128)
                        if cc == 6 and s + 1 < nblk:
                            nback(blkn(s + 1) // 128, hb[(s + 1) % 2][0], hb[(s + 1) % 2][1])
                        for k in range(8):
                            c.op("pe", lambda e: e.matmul(pa_[:, 0:N], lhsT=W[:, k, cc * 128:(cc + 1) * 128], rhs=h_[:, k, 0:N], start=(k == 0), stop=(k == 7)),
                                 reads=[rh, rW[(0, cc // 2)]], writes=[rpa], signal=(k == 7))
                        for k in range(8):
                            c.op("pe", lambda e: e.matmul(pg_[:, 0:N], lhsT=W[:, k, D + cc * 128:D + (cc + 1) * 128], rhs=h_[:, k, 0:N], start=(k == 0), stop=False),
                                 reads=[rh, rW[(1, cc // 2)]], writes=[rpg], signal=False)
                        c.op("pe", lambda e: e.matmul(pg_[:, 0:N], lhsT=ones1[:, :], rhs=pneg[:, s0:s0 + N], start=False, stop=True),
                             reads=[ro1, rpn], writes=[rpg])
                        c.op("act", lambda e: e.activation(out=sg_[:, 0:N], in_=pg_[:, 0:N], func=AF.Sigmoid, bias=b1[:, 8 + cc:9 + cc]),
                             reads=[rpg, rb], writes=[rsg])
                        c.op("dve", lambda e: e.scalar_tensor_tensor(out=st_[:, cc, 0:N], in0=pa_[:, 0:N], scalar=b1[:, cc:cc + 1], in1=sg_[:, 0:N], op0=ALU.add, op1=ALU.mult),
                             reads=[rpa, rb, rsg], writes=[rst])
                    c.dma("sp", "g_st%d" % (s % 2), gview[:, :, s0:s0 + N], st_[:, :, 0:N], reads=[rst])
                c.barrier()

        def prep_conv(ph, j):
            if True:
                W = ph.sb([128, 8, D], BF16)
                rW = Res()
                c.dma("pool", "w_a", W[:], w_pw2[j].rearrange("(k p) n -> p k n", p=128), writes=[rW])
                wdw = ph.sb([128, 8 * 31], F32)
                rwd = Res()
                c.dma("sp", "w_b", wdw[:], wdw_d[j], writes=[rwd])
                bdw = ph.sb([128, 8], F32)
                lng = ph.sb([128, 8], F32)
                lnb = ph.sb([128, 8], F32)
                rsm = Res()
                c.dma("sp", "w_c", bdw[:], bdw_d[j], writes=[rsm])
                c.dma("sp", "w_d", lng[:], lng_d[j], writes=[rsm])
                c.dma("sp", "w_e", lnb[:], lnb_d[j], writes=[rsm])
                b2 = ph.sb([1, D], BF16)
                rb2 = Res()
                c.dma("pool", "w_f", b2[:], bpw2_d[j][0:1, :], writes=[rb2])
                ones1 = ph.sb([1, 128], BF16)
                ro1 = Res()
                c.op("pool", lambda e: e.memset(ones1[:], 1.0), writes=[ro1])
                diag = ph.sb([128, 8 * 31, 128], BF16)
                rdg = Res()
                for i in range(8 * 31):
                    c.op("dve", lambda e: e.tensor_scalar(out=diag[:, i, :], in0=ident[:], scalar1=wdw[:, i:i + 1], scalar2=None, op0=ALU.mult),
                         reads=[r_ident, rwd], writes=[rdg])
                omean = ph.sb([128, 128], BF16)
                rom = Res()
                c.op("pool", lambda e: e.memset(omean[:], 1.0 / 1024), writes=[rom])
                return {'W': W, 'rW': rW, 'wdw': wdw, 'rwd': rwd, 'bdw': bdw, 'lng': lng, 'lnb': lnb, 'rsm': rsm, 'b2': b2, 'rb2': rb2, 'ones1': ones1, 'ro1': ro1, 'diag': diag, 'rdg': rdg, 'omean': omean, 'rom': rom}

        def run_conv(ph, w_, j, x_src, x_dst, t_lo=0, t_hi=None):
            t_hi = NT if t_hi is None else t_hi
            if True:
                W, rW, wdw, rwd, bdw, lng, lnb, rsm, b2, rb2, ones1, ro1, diag, rdg, omean, rom = (w_['W'], w_['rW'], w_['wdw'], w_['rwd'], w_['bdw'], w_['lng'], w_['lnb'], w_['rsm'], w_['b2'], w_['rb2'], w_['ones1'], w_['ro1'], w_['diag'], w_['rdg'], w_['omean'], w_['rom'])
                gb = ph.sbs(3, [128, 8, 542], BF16)
                cf = [(ph.sb([128, 8, 512], F32), [Res() for _ in range(8)]) for _ in range(2)]
                cb = ph.sbs(1, [128, 8, 512], BF16)
                cq = ph.sbs(1, [128, 8, 512], BF16)
                mean = ph.sbs(2, [128, 512], F32)
                rstd = ph.sbs(2, [128, 512], F32)
                zt = [(ph.sb([128, 8, 512], BF16), [Res() for _ in range(8)]) for _ in range(2)]
                xs = ph.sbs(2, [128, D], F32)
                xo = ph.sbs(2, [128, D], F32)
                pc = ph.pss(2, [128, 512], F32)
                pm = ph.pss(1, [128, 512], F32)
                pq = ph.pss(1, [128, 512], F32)
                pY = ph.pss(2, [128, D], F32)
                gview = gl_d.rearrange("(k p) t -> p k t", p=128)
                tk0 = t_lo * 128
                ntok = (t_hi - t_lo) * 128
                nb = (ntok + 511) // 512

                def blk_n(s):
                    return min(512, tk0 + ntok - (tk0 + s * 512))

                def st_l(s):
                    g_, rg = gb[s % 3]
                    N = blk_n(s)
                    lo = tk0 + s * 512 - 15
                    hi = tk0 + s * 512 + N + 15
                    clo = max(lo, 0)
                    chi = min(hi, T)
                    if clo > lo:
                        c.op("pool", lambda e: e.memset(g_[:, :, 0:clo - lo], 0.0), writes=[rg])
                    if chi < hi:
                        c.op("pool", lambda e: e.memset(g_[:, :, chi - lo:N + 30], 0.0), writes=[rg])
                    c.dma("sp", "c_g%d" % (s % 3), g_[:, :, clo - lo:chi - lo], gview[:, :, clo:chi], writes=[rg])

                def st_a(s):
                    g_, rg = gb[s % 3]
                    N = blk_n(s)
                    cf_, rcfs = cf[s % 2]
                    cb_, rcb = cb[0]
                    cq_, rcq = cq[0]
                    for cc in range(8):
                        p_, rp = pc[cc % 2]
                        for k in range(31):
                            c.op("pe", lambda e: e.matmul(p_[:, 0:N], lhsT=diag[:, cc * 31 + k, :], rhs=g_[:, cc, k:k + N], start=(k == 0), stop=(k == 30)),
                                 reads=[rg, rdg], writes=[rp], signal=(k == 30))
                        c.op("act", lambda e: e.activation(out=cf_[:, cc, 0:N], in_=p_[:, 0:N], func=AF.Identity, bias=bdw[:, cc:cc + 1]),
                             reads=[rp, rsm], writes=[rcfs[cc]])
                        c.op("act", lambda e: e.activation(out=cq_[:, cc, 0:N], in_=p_[:, 0:N], func=AF.Square, bias=bdw[:, cc:cc + 1]),
                             reads=[rp, rsm], writes=[rcq])
                        c.op("pool", lambda e: e.tensor_copy(out=cb_[:, cc, 0:N], in_=cf_[:, cc, 0:N]), reads=[rcfs[cc]], writes=[rcb])
                    pm_, rpm = pm[0]
                    pq_, rpq = pq[0]
                    for cc in range(8):
                        c.op("pe", lambda e: e.matmul(pq_[:, 0:N], lhsT=omean[:], rhs=cq_[:, cc, 0:N], start=(cc == 0), stop=(cc == 7)),
                             reads=[rcq, rom], writes=[rpq], signal=(cc == 7))
                    for cc in range(8):
                        c.op("pe", lambda e: e.matmul(pm_[:, 0:N], lhsT=omean[:], rhs=cb_[:, cc, 0:N], start=(cc == 0), stop=(cc == 7)),
                             reads=[rcb, rom], writes=[rpm], signal=(cc == 7))
                    mn_, rmn = mean[s % 2]
                    rs_, rrs = rstd[s % 2]
                    c.op("dve", lambda e: e.tensor_copy(out=mn_[:, 0:N], in_=pm_[:, 0:N]), reads=[rpm], writes=[rmn])
                    c.op("dve", lambda e: e.tensor_tensor(out=rs_[:, 0:N], in0=mn_[:, 0:N], in1=mn_[:, 0:N], op=ALU.mult), reads=[rmn], writes=[rrs])
                    c.op("dve", lambda e: e.tensor_tensor(out=rs_[:, 0:N], in0=pq_[:, 0:N], in1=rs_[:, 0:N], op=ALU.subtract), reads=[rpq, rrs], writes=[rrs])
                    c.op("dve", lambda e: e.tensor_scalar(out=rs_[:, 0:N], in0=rs_[:, 0:N], scalar1=0.0, scalar2=1e-5, op0=ALU.max, op1=ALU.add),
                         reads=[rrs], writes=[rrs])
                    c.op("act", lambda e: e.activation(out=rs_[:, 0:N], in_=rs_[:, 0:N], func=AF.Sqrt), reads=[rrs], writes=[rrs])
                    c.op("dve", lambda e: e.reciprocal(rs_[:, 0:N], rs_[:, 0:N]), reads=[rrs], writes=[rrs])

                def st_b(s):
                    N = blk_n(s)
                    cf_, rcfs = cf[s % 2]
                    mn_, rmn = mean[s % 2]
                    rs_, rrs = rstd[s % 2]
                    zt_, rzts = zt[s % 2]
                    for cc in range(8):
                        c.op("dve", lambda e: e.tensor_tensor(out=cf_[:, cc, 0:N], in0=cf_[:, cc, 0:N], in1=mn_[:, 0:N], op=ALU.subtract),
                             reads=[rcfs[cc], rmn], writes=[rcfs[cc]])
                        c.op("dve", lambda e: e.tensor_tensor(out=cf_[:, cc, 0:N], in0=cf_[:, cc, 0:N], in1=rs_[:, 0:N], op=ALU.mult),
                             reads=[rcfs[cc], rrs], writes=[rcfs[cc]])
                        c.op("act", lambda e: e.activation(out=zt_[:, cc, 0:N], in_=cf_[:, cc, 0:N], func=AF.Silu, scale=lng[:, cc:cc + 1], bias=lnb[:, cc:cc + 1]),
                             reads=[rcfs[cc], rsm], writes=[rzts[cc]])

                def st_c(s):
                    zt_, rzts = zt[s % 2]
                    for tl in range(blk_n(s) // 128):
                        nx = s * 4 + tl
                        tok0 = tk0 + s * 512 + tl * 128
                        x_, rx = xs[nx % 2]
                        xo_, rxo = xo[nx % 2]
                        py_, rpy = pY[nx % 2]
                        c.dma("sp", "c_x%d" % (nx % 2), x_[:], x_src[tok0:tok0 + 128, :], writes=[rx])
                        for half in range(2):
                            for k in range(8):
                                c.op("pe", lambda e: e.matmul(py_[:, half * 512:(half + 1) * 512], lhsT=zt_[:, k, tl * 128:(tl + 1) * 128],
                                                              rhs=W[:, k, half * 512:(half + 1) * 512], start=(k == 0), stop=False),
                                     reads=[rzts[k], rW], writes=[rpy], signal=False)
                            c.op("pe", lambda e: e.matmul(py_[:, half * 512:(half + 1) * 512], lhsT=ones1[:, :], rhs=b2[:, half * 512:(half + 1) * 512],
                                                          start=False, stop=True),
                                 reads=[ro1, rb2], writes=[rpy], signal=(half == 1))
                        c.op("dve", lambda e: e.tensor_tensor(out=xo_[:], in0=py_[:], in1=x_[:], op=ALU.add), reads=[rpy, rx], writes=[rxo])
                        c.dma("pool", "c_st%d" % (nx % 2), x_dst[tok0:tok0 + 128, :], xo_[:], reads=[rxo])

                run_pipe(nb, [(st_l, 0), (st_a, 1), (st_b, 2), (st_c, 3)])
                c.barrier()

        def prep_ffn(ph, i):
            if True:
                Wu = ph.sb([128, 8, 2 * DFF], BF16)
                wv = w_up[i].rearrange("(k p) n -> p k n", p=128)
                cblocks = [(0, 2), (2, 6), (6, 14), (14, 22)]
                rWu = {}
                for (c0, c1) in cblocks:
                    for hf in range(2):
                        r_ = Res()
                        lo_, hi_ = hf * DFF + c0 * 128, hf * DFF + c1 * 128
                        c.dma("pool", "wu%d_%d" % (c0, hf), Wu[:, :, lo_:hi_], wv[:, :, lo_:hi_], writes=[r_])
                        for ch in range(c0, c1):
                            rWu[hf * 22 + ch] = r_
                Wd = ph.sb([128, 22, D], BF16)
                rWd = {}
                wdv = w_down[i].rearrange("(k p) n -> p k n", p=128)
                for k0 in range(0, 22, 2):
                    r_ = Res()
                    c.dma("pool", "wd%d" % k0, Wd[:, k0:k0 + 2, :], wdv[:, k0:k0 + 2, :], writes=[r_])
                    rWd[k0] = r_
                    rWd[k0 + 1] = r_
                fdw = ph.sb([128, 44 * 3], F32)
                fdb = ph.sb([128, 44], F32)
                rf = Res()
                c.dma("sp", "w_c", fdw[:], fdw_d[i], writes=[rf])
                c.dma("sp", "w_d", fdb[:], fdb_d[i], writes=[rf])
                return {'Wu': Wu, 'rWu': rWu, 'Wd': Wd, 'rWd': rWd, 'fdw': fdw, 'fdb': fdb, 'rf': rf}

        def run_ffn(ph, w_, i, x_src, x_dst, tok_lo=0, tok_hi=None):
            tok_hi = T if tok_hi is None else tok_hi
            if True:
                Wu, rWu, Wd, rWd, fdw, fdb, rf = (w_['Wu'], w_['rWu'], w_['Wd'], w_['rWd'], w_['fdw'], w_['fdb'], w_['rf'])
                hb = ph.sbs(2, [128, 8, 512], BF16)
                tt = ph.sbs(4, [128, 512], F32)
                sgl = ph.sbs(2, [128, 512], F32)
                yT = ph.sbs(1, [128, 22, 512], BF16)
                xs = ph.sbs(2, [128, D], F32)
                xo = ph.sbs(2, [128, D], F32)
                pU = ph.pss(4, [128, 512], F32)
                pY = ph.pss(2, [128, D], F32)
                hview = hT_d.rearrange("(k p) t -> p k t", p=128)
                nblk = (tok_hi - tok_lo + FB - 1) // FB
                nu = 0
                nx = 0
                y_, ry = yT[0]
                def load_h(b):
                    s0 = tok_lo + b * FB
                    L = min(FB, tok_hi - s0)
                    h_, rh = hb[b % 2]
                    lo = s0 - 1
                    hi = s0 + L + 1
                    clo = max(lo, 0)
                    chi = min(hi, T)
                    if clo > lo:
                        c.op("pool", lambda e: e.memset(h_[:, :, 0:1], 0.0), writes=[rh])
                    if chi < hi:
                        c.op("pool", lambda e: e.memset(h_[:, :, L + 1:L + 2], 0.0), writes=[rh])
                    c.dma("sp", "f_h%d" % (b % 2), h_[:, :, clo - lo:chi - lo], hview[:, :, clo:chi], writes=[rh])

                load_h(0)
                for b in range(nblk):
                    s0 = tok_lo + b * FB
                    L = min(FB, tok_hi - s0)
                    h_, rh = hb[b % 2]
                    if b + 1 < nblk:
                        load_h(b + 1)
                    for pi in range(22):
                        tg = None
                        for which in range(2):
                            ch = which * 22 + pi
                            p_, rp = pU[nu % 4]
                            t_, rt = tt[nu % 4]
                            nu += 1
                            for k in range(8):
                                c.op("pe", lambda e: e.matmul(p_[:, 0:L + 2], lhsT=Wu[:, k, ch * 128:(ch + 1) * 128], rhs=h_[:, k, 0:L + 2],
                                                              start=(k == 0), stop=(k == 7)),
                                     reads=[rh, rWu[ch]], writes=[rp], signal=(k == 7))
                            c.op("act", lambda e: e.activation(out=t_[:, 0:L], in_=p_[:, 1:L + 1], func=AF.Identity,
                                                               scale=fdw[:, ch * 3 + 1:ch * 3 + 2], bias=fdb[:, ch:ch + 1]),
                                 reads=[rp, rf], writes=[rt])
                            c.op("dve", lambda e: e.scalar_tensor_tensor(out=t_[:, 0:L], in0=p_[:, 0:L], scalar=fdw[:, ch * 3:ch * 3 + 1], in1=t_[:, 0:L],
                                                                         op0=ALU.mult, op1=ALU.add),
                                 reads=[rp, rf, rt], writes=[rt])
                            c.op("dve", lambda e: e.scalar_tensor_tensor(out=t_[:, 0:L], in0=p_[:, 2:L + 2], scalar=fdw[:, ch * 3 + 2:ch * 3 + 3], in1=t_[:, 0:L],
                                                                         op0=ALU.mult, op1=ALU.add),
                                 reads=[rp, rf, rt], writes=[rt])
                            if which == 0:
                                tg = (t_, rt)
                            else:
                                sg_, rsg = sgl[pi % 2]
                                c.op("act", lambda e: e.activation(out=sg_[:, 0:L], in_=tg[0][:, 0:L], func=AF.Silu), reads=[tg[1]], writes=[rsg])
                                c.op("dve", lambda e: e.tensor_tensor(out=y_[:, pi, 0:L], in0=sg_[:, 0:L], in1=t_[:, 0:L], op=ALU.mult),
                                     reads=[rsg, rt], writes=[ry])
                    ntl = (L + 127) // 128
                    for tl in range(ntl):
                        M = min(128, L - tl * 128)
                        tok0 = s0 + tl * 128
                        x_, rx = xs[nx % 2]
                        xo_, rxo = xo[nx % 2]
                        py_, rpy = pY[nx % 2]
                        c.dma("sp", "f_x%d" % (nx % 2), x_[0:M, :], x_src[tok0:tok0 + M, :], writes=[rx])
                        for half in range(2):
                            for k in range(22):
                                c.op("pe", lambda e: e.matmul(py_[0:M, half * 512:(half + 1) * 512], lhsT=y_[:, k, tl * 128:tl * 128 + M],
                                                              rhs=Wd[:, k, half * 512:(half + 1) * 512], start=(k == 0), stop=(k == 21)),
                                     reads=[ry, rWd[k]], writes=[rpy], signal=(half == 1 and k == 21))
                        c.op("dve", lambda e: e.tensor_tensor(out=xo_[0:M, :], in0=py_[0:M, :], in1=x_[0:M, :], op=ALU.add), reads=[rpy, rx], writes=[rxo])
                        c.dma("pool", "f_st%d" % (nx % 2), x_dst[tok0:tok0 + M, :], xo_[0:M, :], reads=[rxo])
                        nx += 1
                c.barrier()

        def phase_final(x_src):
            with Phase(nc, "fin") as ph:
                gbc = ph.sb([128, D], F32)
                r_g = Res()
                c.dma("sp", "w_a", gbc[:], gfin_d[:, :], writes=[r_g])
                NS = 8
                xs = ph.sbs(NS, [128, D], F32)
                junk = ph.sbs(2, [128, D], BF16)
                ss = ph.sbs(NS, [128, 1], F32)
                rs = ph.sbs(NS, [128, 1], F32)
                ys = ph.sbs(3, [128, D], F32)
                t0 = out_tile0

                def st_l(ti):
                    t = t0 + ti
                    x_, rx = xs[ti % NS]
                    c.dma("sp", "n_x%d" % (ti % NS), x_[:], x_src[t * 128:(t + 1) * 128, :], writes=[rx])

                def st_a(ti):
                    x_, rx = xs[ti % NS]
                    j_, rj = junk[ti % 2]
                    s_, rss = ss[ti % NS]
                    q_, rrs = rs[ti % NS]
                    c.op("pool", lambda e: e.memset(s_[:], 0.0), writes=[rss])
                    c.op("act", lambda e: e.activation(out=j_[:], in_=x_[:], func=AF.Square, scale=1.0 / 32, accum_out=s_[:]),
                         reads=[rx], writes=[rj, rss])
                    c.op("dve", lambda e: e.tensor_scalar(out=s_[:], in0=s_[:], scalar1=1e-6, scalar2=None, op0=ALU.add),
                         reads=[rss], writes=[rss])
                    c.op("pool", lambda e: e.tensor_tensor(out=q_[:], in0=s_[:], in1=mhalf[:, 0:1], op=ALU.pow),
                         reads=[rss, r_mhalf], writes=[rrs])

                def st_b(ti):
                    x_, rx = xs[ti % NS]
                    q_, rrs = rs[ti % NS]
                    y_, ry = ys[ti % 3]
                    c.op("dve", lambda e: e.scalar_tensor_tensor(out=y_[:], in0=x_[:], scalar=q_[:], in1=gbc[:], op0=ALU.mult, op1=ALU.mult),
                         reads=[rx, rrs, r_g], writes=[ry])
                    c.dma("pool", "fin_st%d" % (ti % 3), y_out[ti * 128:(ti + 1) * 128, :], y_[:], reads=[ry])

                run_pipe(n_final_tiles, [(st_l, 0), (st_a, 3), (st_b, 5)])
                c.barrier()

        def phase_dump(x_src):
            with Phase(nc, "dump") as ph:
                xs = ph.sbs(2, [128, D], F32)
                for t in range(NT):
                    x_, rx = xs[t % 2]
                    c.dma("sp", "d_x%d" % (t % 2), x_[:], x_src[t * 128:(t + 1) * 128, :], writes=[rx])
                    c.dma("sp", "d_s%d" % (t % 2), y_out[t * 128:(t + 1) * 128, :], x_[:], reads=[rx])
                c.barrier()

        state = {"cur": xw, "nxt": xa}

        def swap():
            cur, nxt = state["cur"], state["nxt"]
            state["cur"], state["nxt"] = nxt, (xb if nxt is xa else xa)

        steps = []
        full = (wr == WR and nphases is None)
        RNG = {"attn2": (3, 58), "nf2": (3, 58), "ffn2": (4, 57), "nm3": (4, 57), "glu3": (4, 57), "conv3": (5, 56), "nf3": (5, 56), "ffn3": (6, 55)}

        if dbg_rng is not None:
            RNG, full = dbg_rng, True

        def rng(key):
            return RNG[key] if full and key in RNG else (0, NT)

        for i in range(DEPTH):
            j = i // 2
            if i % 2 == 0:
                steps.append(("qkv", lambda ph, j=j: prep_qkv(ph, j), lambda ph, w, j=j, i=i: run_qkv(ph, w, j, state["cur"], gmix_d[i])))
                steps.append(("attn", lambda ph, j=j: prep_attn(ph, j), lambda ph, w, j=j, i=i: (run_attn(ph, w, j, state["cur"], state["nxt"], gffn_d[i], *rng("attn%d" % i)), swap())))
            else:
                steps.append(("glu", lambda ph, j=j: prep_glu(ph, j), lambda ph, w, j=j, i=i: run_glu(ph, w, j, state["cur"], gmix_d[i], *rng("glu%d" % i))))
                steps.append(("conv", lambda ph, j=j: prep_conv(ph, j), lambda ph, w, j=j, i=i: (run_conv(ph, w, j, state["cur"], state["nxt"], *rng("conv%d" % i)), swap())))
            if i % 2 == 1:
                steps.append(("norm", None, lambda ph, w, i=i: phase_norm(ph, state["cur"], gffn_d[i], True, "f%d" % i, *rng("nf%d" % i))))
            steps.append(("ffn", lambda ph, i=i: prep_ffn(ph, i), lambda ph, w, i=i: (run_ffn(ph, w, i, state["cur"], state["nxt"], rng("ffn%d" % i)[0] * 128, rng("ffn%d" % i)[1] * 128), swap())))
        if nphases is not None:
            steps = steps[:nphases]
        can_prefetch_during = set()
        phs = [None] * len(steps)
        ws = [None] * len(steps)

        def do_prep(k):
            name, prep, run = steps[k]
            phs[k] = Phase(nc, "%s%d" % (name, k), side=("left" if k % 2 == 0 else "right"))
            phs[k].__enter__()
            ws[k] = prep(phs[k]) if prep is not None else None

        for k in range(len(steps)):
            if phs[k] is None:
                do_prep(k)
            if k + 1 < len(steps) and steps[k][0] in can_prefetch_during and steps[k + 1][1] is not None:
                do_prep(k + 1)
            steps[k][2](phs[k], ws[k])
            phs[k].__exit__(None, None, None)
        if nphases is None:
            phase_final(state["cur"])
        else:
            phase_dump(state["cur"])
        c.barrier()
        c.close()
    return nc


def _row_info():
    seq = -np.ones(GROWS, np.int64)
    rr = -np.ones(GROWS, np.int64)
    for s, off in enumerate(SEQ_OFF):
        seq[off:off + ROWS] = s
        rr[off:off + ROWS] = np.arange(ROWS)
    return seq, rr


def _core_tables(core, seq, rr, g0=None, WR=WR, ROWS=ROWS):
    GROWS = seq.shape[0]
    NT = WR // 2
    T = WR * GW
    if g0 is None:
        g0 = core * CORE_ROWS - HALO_B
    grow = g0 + np.arange(WR)
    inr = (grow >= 0) & (grow < GROWS)
    gs = np.where(inr, seq[np.clip(grow, 0, GROWS - 1)], -1)
    gr = np.where(inr, rr[np.clip(grow, 0, GROWS - 1)], -1)
    valid_row = gs >= 0
    tokvalid = np.repeat(valid_row, GW)
    tokmask = tokvalid.reshape(NT, 128).T.astype(np.float32).copy()
    padneg = np.where(tokvalid, 0.0, NEG).astype(np.float32)[None, :]
    rb = np.zeros((128, NT, 7, 2), np.float32)
    for m in range(NT):
        for qrl in range(2):
            qrow = 2 * m + qrl
            for oi, o in enumerate(OFFS):
                for krl in range(2):
                    krow = 2 * (m + o) + krl
                    ok = True
                    if valid_row[qrow]:
                        if krow < 0 or krow >= WR or gs[krow] != gs[qrow]:
                            ok = False
                        else:
                            r = gr[qrow]
                            rs = min(max(r - 4, 0), ROWS - 8)
                            ok = rs <= gr[krow] < rs + 8
                    else:
                        ok = (krow == qrow)
                    if not ok:
                        rb[krl * 64:(krl + 1) * 64, m, oi, qrl] = NEG
    return grow, inr, gs, gr, tokmask, padneg, rb.reshape(128, NT * 14)


def _needed_offsets():
    seq, rr = _row_info()
    need = np.zeros((NT, 7), bool)
    for core in range(NCORE):
        rb = _core_tables(core, seq, rr)[6].reshape(128, NT, 7, 2)
        need |= (rb == 0).any(axis=(0, 3))
    return [set(np.nonzero(need[m])[0].tolist()) for m in range(NT)]


def _mergeable_q(tables=None):
    if tables is None:
        seq, rr = _row_info()
        tables = [_core_tables(core, seq, rr)[6] for core in range(NCORE)]
    nt = tables[0].shape[1] // 14
    same = np.ones((nt, 7), bool)
    for rb in tables:
        rb = rb.reshape(128, nt, 7, 2)
        same &= (rb[:, :, :, 0] == rb[:, :, :, 1]).all(axis=0)
    return {(m, oi) for m in range(nt) for oi in range(7) if same[m, oi]}


def _bias_table(rpb):
    kc = np.arange(GW)[:, None]
    qc = np.arange(GW)[None, :]
    ws = np.clip(qc - 8, 0, GW - 16)
    col_ok = (kc >= ws) & (kc < ws + 16)
    dc = np.clip(kc - qc + 15, 0, 30)
    tab = np.full((2, GW, 7, NH, 2, GW), NEG, np.float32)
    for oi, o in enumerate(OFFS):
        for krl in range(2):
            for qrl in range(2):
                delta = 2 * o + krl - qrl
                if delta < -7 or delta > 7:
                    continue
                blk = rpb[:, delta + 7, :][:, dc]
                blk = np.where(col_ok[None], blk, np.float32(NEG))
                tab[krl, :, oi, :, qrl, :] = blk.transpose(1, 0, 2)
    order = [h for g in HGRP for h in g]
    tab = tab[:, :, :, order]
    return tab.reshape(128, 7 * NH * 128)


def _pc(v, nchunk):
    return np.ascontiguousarray(v.reshape(nchunk, 128).T)


def _bc(v):
    return np.ascontiguousarray(np.broadcast_to(v[None, :], (128, v.shape[0])))


_NC_CACHE = {}


def kernel(x_prompt, x_sample, attn_w_qkv, attn_b_qkv, attn_rpb, attn_w_o, attn_b_o,
           conv_w_pw1, conv_b_pw1, conv_w_dw, conv_b_dw, conv_ln_g, conv_ln_b, conv_w_pw2, conv_b_pw2,
           ffn_w_up, ffn_w_dw, ffn_b_dw, ffn_w_down, norm_mix, norm_ffn, norm_final):
    f = lambda a: np.asarray(a, dtype=np.float32)
    x_prompt, x_sample = f(x_prompt), f(x_sample)
    seq, rr = _row_info()
    xg = np.zeros((GROWS, GW, D), np.float32)
    seqs = [x_prompt[0], x_prompt[1], x_sample[0]]
    for s, off in enumerate(SEQ_OFF):
        xg[off:off + ROWS] = seqs[s].reshape(ROWS, GW, D)

    shared = {
        "ident": np.eye(128, dtype=np.float32),
        "w_qkv": f(attn_w_qkv),
        "bqk": np.stack([_pc(f(attn_b_qkv)[j, :2 * D], 16) for j in range(2)]),
        "bv_bc": np.stack([_bc(f(attn_b_qkv)[j, 2 * D:]) for j in range(2)]),
        "tab": np.stack([_bias_table(f(attn_rpb)[j]) for j in range(2)]),
        "w_o": f(attn_w_o),
        "bo_bc": np.stack([_bc(f(attn_b_o)[j]) for j in range(2)]),
        "w_pw1": f(conv_w_pw1),
        "bpw1": np.stack([_pc(f(conv_b_pw1)[j], 16) for j in range(2)]),
        "wdw": np.stack([np.ascontiguousarray(f(conv_w_dw)[j].T.reshape(8, 128, 31).transpose(1, 0, 2).reshape(128, 8 * 31)) for j in range(2)]),
        "bdw": np.stack([_pc(f(conv_b_dw)[j], 8) for j in range(2)]),
        "lng": np.stack([_pc(f(conv_ln_g)[j], 8) for j in range(2)]),
        "lnb": np.stack([_pc(f(conv_ln_b)[j], 8) for j in range(2)]),
        "w_pw2": f(conv_w_pw2),
        "bpw2_bc": np.stack([_bc(f(conv_b_pw2)[j]) for j in range(2)]),
        "w_up": f(ffn_w_up),
        "fdw": np.stack([np.ascontiguousarray(f(ffn_w_dw)[i].T.reshape(44, 128, 3).transpose(1, 0, 2).reshape(128, 44 * 3)) for i in range(4)]),
        "fdb": np.stack([_pc(f(ffn_b_dw)[i], 44) for i in range(4)]),
        "w_down": f(ffn_w_down),
        "gmix_bc": np.stack([_bc(f(norm_mix)[i]) for i in range(4)]),
        "gffn_bc": np.stack([_bc(f(norm_ffn)[i]) for i in range(4)]),
        "gfin_bc": _bc(f(norm_final)),
    }
    in_maps = []
    for core in range(NCORE):
        grow, inr, gs, gr, tokmask, padneg, rowbias = _core_tables(core, seq, rr)
        xwin = np.zeros((WR, GW, D), np.float32)
        xwin[inr] = xg[grow[inr]]
        m = dict(shared)
        m["xw"] = xwin.reshape(T, D)
        m["tokmask"] = tokmask
        m["padneg"] = padneg
        m["rowbias"] = rowbias
        in_maps.append(m)

    if "nc" not in _NC_CACHE:
        _NC_CACHE["nc"] = build_program(tile_offs=_needed_offsets(), merge_q=_mergeable_q())
    res = run_bass_kernel_spmd(_NC_CACHE["nc"], in_maps, core_ids=list(range(NCORE)))
    yg = np.zeros((GROWS, GW, D), np.float32)
    for core in range(NCORE):
        yg[core * CORE_ROWS:(core + 1) * CORE_ROWS] = np.asarray(res.results[core]["y"], dtype=np.float32).reshape(CORE_ROWS, GW, D)
    outs = [yg[off:off + ROWS].reshape(ROWS * GW, D) for off in SEQ_OFF]
    y_prompt = np.stack([outs[0], outs[1]]).astype(np.float32)
    y_sample = outs[2][None].astype(np.float32)
    return (y_prompt, y_sample)
```

```python
import contextlib
import os
import numpy as np
import concourse.bass as bass
import concourse.mybir as mybir
from concourse.bass_utils import run_bass_kernel_spmd

F32 = mybir.dt.float32
BF16 = mybir.dt.bfloat16
AF = mybir.ActivationFunctionType
ALU = mybir.AluOpType

NCORE = 8
D = 1024
DFF = 2816
NH = 16
GW = 64
ROWS = 256
GAP = 8
CORE_ROWS = 98
HALO_B = 12
HALO_A = 10
WR = CORE_ROWS + HALO_B + HALO_A
NT = WR // 2
T = WR * GW
SEQ_OFF = [0, ROWS + GAP, 2 * (ROWS + GAP)]
GROWS = NCORE * CORE_ROWS
NEG = -30000.0
OFFS = [-3, -2, -1, 0, 1, 2, 3]
HGRP = [[2 * (4 * (g // 2) + i) + (g % 2) for i in range(4)] for g in range(4)]
FB = 510
DEPTH = 4


class Res:
    __slots__ = ("w", "r")

    def __init__(self):
        self.w = None
        self.r = []


class Ctx:
    def __init__(self, nc):
        self.nc = nc
        self.eng = {"pe": nc.tensor, "act": nc.scalar, "dve": nc.vector, "pool": nc.gpsimd, "sp": nc.sync}
        self.sems = {}
        self.cnt = {}
        self.waited = {k: {} for k in self.eng}
        self._stack = []
        self.ninst = 0
        self.keymap = {}

    def sem(self, key):
        if key not in self.sems:
            cm = self.nc.semaphore("s_" + key)
            self.sems[key] = cm.__enter__()
            self._stack.append(cm)
            self.cnt[key] = 0
        return self.sems[key]

    def _need(self, e, deps):
        w = self.waited[e]
        best = {}
        for (k, v) in deps:
            if w.get(k, 0) < v and best.get(k, 0) < v:
                best[k] = v
        for k, v in best.items():
            self.eng[e].wait_ge(self.sems[k], v)
            w[k] = v

    def _deps(self, e, reads, writes):
        deps = []
        for r in reads:
            if r.w is not None:
                deps.append(r.w)
        for x in writes:
            if x.w is not None:
                deps.append(x.w)
            deps.extend(x.r)
        if e == "pe":
            deps = [d for d in deps if d[0] != "pe"]
        return deps

    def _record(self, tok, reads, writes):
        for r in reads:
            r.r.append(tok)
            if len(r.r) > 64:
                best = {}
                for (k, v) in r.r:
                    if best.get(k, 0) < v:
                        best[k] = v
                r.r = list(best.items())
        for x in writes:
            x.w = tok
            x.r = []

    def op(self, e, fn, reads=(), writes=(), signal=True):
        self.sem(e)
        self._need(e, self._deps(e, reads, writes))
        inst = fn(self.eng[e])
        self.ninst += 1
        if signal:
            self.cnt[e] += 1
            inst.then_inc(self.sems[e], 1)
            tok = (e, self.cnt[e])
        else:
            tok = (e, self.cnt[e] + 1)
        self._record(tok, reads, writes)
        return inst

    def dma(self, q, semkey, out, in_, reads=(), writes=(), **kw):
        km = self.keymap.setdefault(q, {})
        if semkey not in km:
            km[semkey] = "%s_g%d" % (q, len(km))
        semkey = km[semkey]
        self.sem(semkey)
        self._need(q, self._deps(q, reads, writes))
        inst = self.eng[q].dma_start(out=out, in_=in_, **kw)
        self.ninst += 1
        self.cnt[semkey] += 16
        inst.then_inc(self.sems[semkey], 16)
        self._record((semkey, self.cnt[semkey]), reads, writes)
        return inst

    def barrier(self):
        allt = [(k, v) for k, v in self.cnt.items() if v > 0]
        for e in self.eng:
            self._need(e, allt)
        self.keymap = {}

    def close(self):
        for cm in reversed(self._stack):
            cm.__exit__(None, None, None)


class Phase:
    def __init__(self, nc, name, side=None):
        self.nc = nc
        self.name = name
        self.side = side
        self.es = contextlib.ExitStack()
        self.n = 0

    def __enter__(self):
        self.es.__enter__()
        return self

    def __exit__(self, *a):
        return self.es.__exit__(*a)

    def sb(self, shape, dt):
        self.n += 1
        return self.es.enter_context(self.nc.sbuf_tensor("%s_s%d" % (self.name, self.n), list(shape), dt, side=self.side))

    def ps(self, shape, dt):
        self.n += 1
        return self.es.enter_context(self.nc.psum_tensor("%s_p%d" % (self.name, self.n), list(shape), dt))

    def sbs(self, n, shape, dt):
        return [(self.sb(shape, dt), Res()) for _ in range(n)]

    def pss(self, n, shape, dt):
        return [(self.ps(shape, dt), Res()) for _ in range(n)]


def build_program(wr=WR, out_tile0=HALO_B // 2, n_final_tiles=CORE_ROWS // 2, nphases=None, tile_offs=None, dbg_rng=None, merge_q=None):
    nc = bass.Bass("TRN2", target_bir_lowering=False)
    T = wr * GW
    NT = wr // 2

    def din(name, shape, dt=F32):
        return nc.dram_tensor(name, list(shape), dt, kind="ExternalInput").ap()

    def dscr(name, shape, dt):
        return nc.dram_tensor(name, list(shape), dt, kind="Internal").ap()

    xw = din("xw", [T, D])
    tokmask_d = din("tokmask", [128, NT])
    padneg_d = din("padneg", [1, T])
    rowbias_d = din("rowbias", [128, NT * 7 * 2])
    ident_d = din("ident", [128, 128])
    w_qkv = din("w_qkv", [2, D, 3 * D])
    bqk_d = din("bqk", [2, 128, 16])
    bv_d = din("bv_bc", [2, 128, D])
    tab_d = din("tab", [2, 128, 7 * NH * 128])
    w_o = din("w_o", [2, D, D])
    bo_d = din("bo_bc", [2, 128, D])
    w_pw1 = din("w_pw1", [2, D, 2 * D])
    bpw1_d = din("bpw1", [2, 128, 16])
    wdw_d = din("wdw", [2, 128, 8 * 31])
    bdw_d = din("bdw", [2, 128, 8])
    lng_d = din("lng", [2, 128, 8])
    lnb_d = din("lnb", [2, 128, 8])
    w_pw2 = din("w_pw2", [2, D, D])
    bpw2_d = din("bpw2_bc", [2, 128, D])
    w_up = din("w_up", [4, D, 2 * DFF])
    fdw_d = din("fdw", [4, 128, 44 * 3])
    fdb_d = din("fdb", [4, 128, 44])
    w_down = din("w_down", [4, DFF, D])
    gmix_d = din("gmix_bc", [4, 128, D])
    gffn_d = din("gffn_bc", [4, 128, D])
    gfin_d = din("gfin_bc", [128, D])
    y_out = nc.dram_tensor("y", [(n_final_tiles if nphases is None else NT) * 128, D], F32, kind="ExternalOutput").ap()

    xa = dscr("xa", [T, D], F32)
    xb = dscr("xb", [T, D], F32)
    hT_d = dscr("hT", [D, T], BF16)
    qT_d = dscr("qT", [D, T], BF16)
    kT_d = dscr("kT", [D, T], BF16)
    v_d = dscr("vv", [T, NH * 65], BF16)
    gl_d = dscr("glu", [D, T], BF16)

    c = Ctx(nc)
    glob = contextlib.ExitStack()
    with glob:
        def gsb(name, shape, dt):
            return glob.enter_context(nc.sbuf_tensor(name, list(shape), dt))

        ident = gsb("ident_bf", [128, 128], BF16)
        r_ident = Res()
        tokmask = gsb("tokmask_sb", [128, NT], F32)
        r_tokmask = Res()
        mhalf = gsb("mhalf", [128, 512], F32)
        r_mhalf = Res()
        c.dma("pool", "g_ld", ident[:], ident_d[:, :], writes=[r_ident])
        c.dma("sp", "g_ld2", tokmask[:], tokmask_d[:, :], writes=[r_tokmask])
        c.op("pool", lambda e: e.memset(mhalf[:], -0.5), writes=[r_mhalf])
        c.barrier()

        def run_pipe(n, stages):
            maxlag = max(l for _, l in stages)
            for step in range(n + maxlag):
                for fn, lag in stages:
                    t = step - lag
                    if 0 <= t < n:
                        fn(t)

        def phase_norm(ph, x_src, g_dram, use_mask, tag, t_lo=0, t_hi=None):
            t_hi = NT if t_hi is None else t_hi
            if True:
                gbc = ph.sb([128, D], F32)
                r_g = Res()
                c.dma("sp", "w_a", gbc[:], g_dram, writes=[r_g])
                NS = 8
                xs = ph.sbs(NS, [128, D], F32)
                junk = ph.sbs(2, [128, D], BF16)
                ss = ph.sbs(NS, [128, 1], F32)
                rs = ph.sbs(NS, [128, 1], F32)
                hs = ph.sbs(3, [128, D], BF16)
                hts = ph.sbs(2, [128, 8, 512], BF16)
                pT = ph.pss(4, [128, D], BF16)
                hview = hT_d.rearrange("(k p) t -> p k t", p=128)

                def st_l(i):
                    t = t_lo + i
                    x_, rx = xs[i % NS]
                    c.dma("sp", "n_x%d" % (i % NS), x_[:], x_src[t * 128:(t + 1) * 128, :], writes=[rx])

                def st_a(i):
                    t = t_lo + i
                    x_, rx = xs[i % NS]
                    j_, rj = junk[i % 2]
                    s_, rss = ss[i % NS]
                    q_, rrs = rs[i % NS]
                    c.op("pool", lambda e: e.memset(s_[:], 0.0), writes=[rss])
                    c.op("act", lambda e: e.activation(out=j_[:], in_=x_[:], func=AF.Square, scale=1.0 / 32, accum_out=s_[:]),
                         reads=[rx], writes=[rj, rss])
                    c.op("dve", lambda e: e.tensor_scalar(out=s_[:], in0=s_[:], scalar1=1e-6, scalar2=None, op0=ALU.add),
                         reads=[rss], writes=[rss])
                    c.op("pool", lambda e: e.tensor_tensor(out=q_[:], in0=s_[:], in1=mhalf[:, 0:1], op=ALU.pow),
                         reads=[rss, r_mhalf], writes=[rrs])
                    if use_mask:
                        c.op("pool", lambda e: e.tensor_tensor(out=q_[:], in0=q_[:], in1=tokmask[:, t:t + 1], op=ALU.mult),
                             reads=[rrs, r_tokmask], writes=[rrs])

                def st_b(i):
                    t = t_lo + i
                    x_, rx = xs[i % NS]
                    q_, rrs = rs[i % NS]
                    h_, rh = hs[i % 3]
                    c.op("dve", lambda e: e.scalar_tensor_tensor(out=h_[:], in0=x_[:], scalar=q_[:], in1=gbc[:], op0=ALU.mult, op1=ALU.mult),
                         reads=[rx, rrs, r_g], writes=[rh])
                    p_, rp = pT[i % 4]
                    for k in range(8):
                        c.op("pe", lambda e: e.transpose(out=p_[:, k * 128:(k + 1) * 128], in_=h_[:, k * 128:(k + 1) * 128], identity=ident[:]),
                             reads=[rh, r_ident], writes=[rp], signal=(k == 7))

                def st_c(i):
                    t = t_lo + i
                    p_, rp = pT[i % 4]
                    ht_, rht = hts[(i // 4) % 2]
                    c.op("dve", lambda e: e.tensor_copy(out=ht_[:, :, (i % 4) * 128:(i % 4 + 1) * 128], in_=p_[:].rearrange("p (k t) -> p k t", k=8)),
                         reads=[rp], writes=[rht])
                    if i % 4 == 3 or t == t_hi - 1:
                        g0 = (t_lo + (i // 4) * 4) * 128
                        n = (i % 4 + 1) * 128
                        c.dma("sp", "n_st%d" % ((i // 4) % 2), hview[:, :, g0:g0 + n], ht_[:, :, 0:n], reads=[rht])

                run_pipe(t_hi - t_lo, [(st_l, 0), (st_a, 3), (st_b, 5), (st_c, 6)])
                c.barrier()

        def make_normer(ph, x_src, g_dram, use_mask):
            gbc = ph.sb([128, D], F32)
            r_g = Res()
            c.dma("sp", "nw_g", gbc[:], g_dram, writes=[r_g])
            NS = 8
            xs = ph.sbs(NS, [128, D], F32)
            junk = ph.sbs(2, [128, D], BF16)
            ss = ph.sbs(NS, [128, 1], F32)
            rs = ph.sbs(NS, [128, 1], F32)
            hs = ph.sbs(NS, [128, D], BF16)
            pT = ph.pss(2, [128, D], BF16)
            cnt = {"f": 0, "b": 0}

            def fload(tok0, nt):
                base = cnt["f"]
                cnt["f"] += nt
                for tl in range(nt):
                    i = base + tl
                    x_, rx = xs[i % NS]
                    c.dma("sp", "nw_x%d" % (i % NS), x_[:], x_src[tok0 + tl * 128:tok0 + (tl + 1) * 128, :], writes=[rx])

            cnt["c"] = 0

            def fcomp(tok0, nt):
                base = cnt["c"]
                cnt["c"] += nt
                for tl in range(nt):
                    i = base + tl
                    t = (tok0 // 128) + tl
                    x_, rx = xs[i % NS]
                    j_, rj = junk[i % 2]
                    s_, rss = ss[i % NS]
                    q_, rrs = rs[i % NS]
                    c.op("pool", lambda e: e.memset(s_[:], 0.0), writes=[rss])
                    c.op("act", lambda e: e.activation(out=j_[:], in_=x_[:], func=AF.Square, scale=1.0 / 32, accum_out=s_[:]),
                         reads=[rx], writes=[rj, rss])
                    c.op("dve", lambda e: e.tensor_scalar(out=s_[:], in0=s_[:], scalar1=1e-6, scalar2=None, op0=ALU.add),
                         reads=[rss], writes=[rss])
                    c.op("pool", lambda e: e.tensor_tensor(out=q_[:], in0=s_[:], in1=mhalf[:, 0:1], op=ALU.pow),
                         reads=[rss, r_mhalf], writes=[rrs])
                    if use_mask:
                        c.op("pool", lambda e: e.tensor_tensor(out=q_[:], in0=q_[:], in1=tokmask[:, t:t + 1], op=ALU.mult),
                             reads=[rrs, r_tokmask], writes=[rrs])
                for tl in range(nt):
                    i = base + tl
                    x_, rx = xs[i % NS]
                    q_, rrs = rs[i % NS]
                    h_, rh = hs[i % NS]
                    c.op("dve", lambda e: e.scalar_tensor_tensor(out=h_[:], in0=x_[:], scalar=q_[:], in1=gbc[:], op0=ALU.mult, op1=ALU.mult),
                         reads=[rx, rrs, r_g], writes=[rh])

            def back(nt, dst, rdst):
                base = cnt["b"]
                cnt["b"] += nt
                for tl in range(nt):
                    i = base + tl
                    h_, rh = hs[i % NS]
                    p_, rp = pT[i % 2]
                    for k in range(8):
                        c.op("pe", lambda e: e.transpose(out=p_[:, k * 128:(k + 1) * 128], in_=h_[:, k * 128:(k + 1) * 128], identity=ident[:]),
                             reads=[rh, r_ident], writes=[rp], signal=(k == 7))
                    c.op("dve", lambda e: e.tensor_copy(out=dst[:, :, tl * 128:(tl + 1) * 128], in_=p_[:].rearrange("p (k t) -> p k t", k=8)),
                         reads=[rp], writes=[rdst])

            return fload, fcomp, back

        def prep_qkv(ph, j):
            if True:
                W = ph.sb([128, 8, 3 * D], BF16)
                wv = w_qkv[j].rearrange("(k p) n -> p k n", p=128)
                rW = {}
                for cb in range(6):
                    r_ = Res()
                    c.dma("pool", "wq%d" % cb, W[:, :, cb * 512:(cb + 1) * 512], wv[:, :, cb * 512:(cb + 1) * 512], writes=[r_])
                    rW[cb] = r_
                bqk = ph.sb([128, 16], F32)
                rb = Res()
                c.dma("sp", "w_b", bqk[:], bqk_d[j], writes=[rb])
                c.op("dve", lambda e: e.tensor_scalar(out=bqk[:, 0:8], in0=bqk[:, 0:8], scalar1=0.125, scalar2=None, op0=ALU.mult),
                     reads=[rb], writes=[rb])
                bv = ph.sb([128, D], F32)
                rbv = Res()
                c.dma("sp", "w_c", bv[:], bv_d[j], writes=[rbv])
                return {'W': W, 'rW': rW, 'bqk': bqk, 'rb': rb, 'bv': bv, 'rbv': rbv}

        def run_qkv(ph, w_, j, x_src, g_dram):
            if True:
                nload, ncomp, nback = make_normer(ph, x_src, g_dram, False)
                W, rW, bqk, rb, bv, rbv = (w_['W'], w_['rW'], w_['bqk'], w_['rb'], w_['bv'], w_['rbv'])
                hb = ph.sbs(2, [128, 8, 512], BF16)
                qst = ph.sbs(2, [128, 8, 512], BF16)
                vt = ph.sbs(3, [128, NH, 65], BF16)
                for (v_, rv) in vt:
                    c.op("pool", lambda e: e.memset(v_[:], 1.0), writes=[rv])
                pq = ph.pss(3, [128, 512], F32)
                pv = ph.pss(3, [128, 512], F32)
                hview = hT_d.rearrange("(k p) t -> p k t", p=128)
                qview = qT_d.rearrange("(k p) t -> p k t", p=128)
                kview = kT_d.rearrange("(k p) t -> p k t", p=128)
                npq = 0
                npv = 0
                nv = 0
                nload(0, 4)
                ncomp(0, 4)
                nback(4, hb[0][0], hb[0][1])
                for s in range(T // 512):
                    h_, rh = hb[s % 2]
                    if s + 1 < T // 512:
                        nload((s + 1) * 512, 4)
                    for which in range(2):
                        st_, rst = qst[which]
                        if which == 1 and s + 1 < T // 512:
                            ncomp((s + 1) * 512, 4)
                        for cc in range(8):
                            col = which * D + cc * 128
                            p_, rp = pq[npq % 3]
                            npq += 1
                            for k in range(8):
                                c.op("pe", lambda e: e.matmul(p_[:], lhsT=W[:, k, col:col + 128], rhs=h_[:, k, :], start=(k == 0), stop=(k == 7)),
                                     reads=[rh, rW[col // 512]], writes=[rp], signal=(k == 7))
                            bcol = which * 8 + cc
                            c.op("act", lambda e: e.activation(out=st_[:, cc, :], in_=p_[:], func=AF.Identity,
                                                               scale=(0.125 if which == 0 else 1.0), bias=bqk[:, bcol:bcol + 1]),
                                 reads=[rp, rb], writes=[rst])
                        dst = qview if which == 0 else kview
                        c.dma("sp", "q_st%d" % which, dst[:, :, s * 512:(s + 1) * 512], st_[:], reads=[rst])
                    if s + 1 < T // 512:
                        nback(4, hb[(s + 1) % 2][0], hb[(s + 1) % 2][1])
                    for tl in range(4):
                        v_, rv = vt[nv % 3]
                        nv += 1
                        for half in range(2):
                            p_, rp = pv[npv % 3]
                            npv += 1
                            for k in range(8):
                                c.op("pe", lambda e: e.matmul(p_[:], lhsT=h_[:, k, tl * 128:(tl + 1) * 128],
                                                              rhs=W[:, k, 2 * D + half * 512:2 * D + (half + 1) * 512], start=(k == 0), stop=(k == 7)),
                                     reads=[rh, rW[4 + half]], writes=[rp], signal=(k == 7))
                            c.op("dve", lambda e: e.tensor_tensor(out=v_[:, half * 8:(half + 1) * 8, 0:64],
                                                                  in0=p_[:].rearrange("p (h d) -> p h d", d=64),
                                                                  in1=bv[:, half * 512:(half + 1) * 512].rearrange("p (h d) -> p h d", d=64), op=ALU.add),
                                 reads=[rp, rbv], writes=[rv])
                        tok0 = s * 512 + tl * 128
                        c.dma("sp", "q_sv%d" % ((nv - 1) % 3), v_d[tok0:tok0 + 128, :], v_[:].rearrange("p h d -> p (h d)"), reads=[rv])
                c.barrier()

        def prep_attn(ph, j):
            if True:
                Wo = ph.sb([128, 8, D], BF16)
                rW = Res()
                c.dma("pool", "w_a", Wo[:], w_o[j].rearrange("(k p) n -> p k n", p=128), writes=[rW])
                tab = ph.sb([128, 7, NH, 128], F32)
                rtab = Res()
                tv = tab_d[j].rearrange("p (o h q) -> p o h q", o=7, h=NH)
                for o in range(7):
                    c.dma("sp", "w_b", tab[:, o, :, :], tv[:, o, :, :], writes=[rtab])
                rowb = ph.sb([128, NT * 14], F32)
                rrow = Res()
                c.dma("sp", "w_c", rowb[:], rowbias_d[:, :], writes=[rrow])
                bo = ph.sb([1, D], BF16)
                rbo = Res()
                c.dma("pool", "w_d", bo[:], bo_d[j][0:1, :], writes=[rbo])
                ones1 = ph.sb([1, 128], BF16)
                ro1 = Res()
                c.op("pool", lambda e: e.memset(ones1[:], 1.0), writes=[ro1])
                return {'Wo': Wo, 'rW': rW, 'tab': tab, 'rtab': rtab, 'rowb': rowb, 'rrow': rrow, 'bo': bo, 'rbo': rbo, 'ones1': ones1, 'ro1': ro1}

        def run_attn(ph, w_, j, x_src, x_dst, gn_dram, q_lo=0, q_hi=None):
            q_hi = NT if q_hi is None else q_hi
            if True:
                Wo, rW, tab, rtab, rowb, rrow, bo, rbo, ones1, ro1 = (w_['Wo'], w_['rW'], w_['tab'], w_['rtab'], w_['rowb'], w_['rrow'], w_['bo'], w_['rbo'], w_['ones1'], w_['ro1'])
                qs = ph.sbs(3, [128, 8, 128], BF16)
                NR = 10
                kr = ph.sbs(NR, [128, 8, 128], BF16)
                vr = ph.sbs(NR, [128, NH * 65], BF16)
                xs = ph.sbs(4, [128, D], F32)
                xo = ph.sbs(3, [128, D], F32)
                gbc = ph.sb([128, D], F32)
                r_g = Res()
                c.dma("sp", "w_g", gbc[:], gn_dram, writes=[r_g])
                nss = ph.sbs(3, [128, 1], F32)
                nrs = ph.sbs(3, [128, 1], F32)
                nh = ph.sbs(2, [128, D], BF16)
                nhts = ph.sbs(1, [128, 8, 512], BF16)
                njunk = ph.sbs(1, [128, D], BF16)
                hview = hT_d.rearrange("(k p) t -> p k t", p=128)
                sf = ph.sbs(3, [128, 512], F32)
                pt = ph.sbs(2, [128, 7, 512], BF16)
                osb = ph.sbs(2, [128, D], BF16)
                ot = ph.sbs(2, [128, 8, 128], BF16)
                rsum = ph.sbs(2, [128, 4], F32)
                pS = ph.pss(3, [128, 512], F32)
                pO = ph.pss(2, [128, 512], F32)
                pT = ph.pss(1, [128, D], BF16)
                pY = ph.pss(1, [128, D], F32)
                qview = qT_d.rearrange("(k p) t -> p k t", p=128)
                kview = kT_d.rearrange("(k p) t -> p k t", p=128)

                def load_kv(jc):
                    k_, rk = kr[jc % NR]
                    v_, rv = vr[jc % NR]
                    c.dma("sp", "a_k%d" % (jc % NR), k_[:], kview[:, :, jc * 128:(jc + 1) * 128], writes=[rk])
                    c.dma("sp", "a_v%d" % (jc % NR), v_[:], v_d[jc * 128:(jc + 1) * 128, :], writes=[rv])

                for jc in range(max(0, q_lo - 3), min(NT, q_lo + 3)):
                    load_kv(jc)
                cnt = {"S": 0, "Sf": 0}
                units = [(m, hg) for m in range(q_lo, q_hi) for hg in range(4)]

                def valid_of(m):
                    return [(oi, m + o) for oi, o in enumerate(OFFS)
                            if 0 <= m + o < NT and (tile_offs is None or oi in tile_offs[m])]

                def load_tile(m):
                    if m + 3 < NT:
                        load_kv(m + 3)
                    q_, rq = qs[m % 3]
                    c.dma("sp", "a_q%d" % (m % 3), q_[:], qview[:, :, m * 128:(m + 1) * 128], writes=[rq])
                    x_, rx = xs[m % 4]
                    c.dma("sp", "a_x%d" % (m % 4), x_[:], x_src[m * 128:(m + 1) * 128, :], writes=[rx])

                load_tile(q_lo)

                def st_s(u):
                    m, hg = units[u]
                    if hg == 0 and m + 1 < q_hi:
                        load_tile(m + 1)
                    q_, rq = qs[m % 3]
                    pt_, rpt = pt[u % 2]
                    for (oi, jc) in valid_of(m):
                        k_, rk = kr[jc % NR]
                        ps_, rps = pS[cnt["S"] % 3]
                        cnt["S"] += 1
                        for hh in range(4):
                            h = HGRP[hg][hh]
                            cc = h // 2
                            pb = (h % 2) * 64
                            c.op("pe", lambda e: e.matmul(ps_[:, hh * 128:(hh + 1) * 128], lhsT=k_[pb:pb + 64, cc, :], rhs=q_[pb:pb + 64, cc, :],
                                                          start=True, stop=True),
                                 reads=[rk, rq], writes=[rps], signal=(hh == 3))
                        sf_, rsf = sf[cnt["Sf"] % 3]
                        cnt["Sf"] += 1
                        if merge_q is not None and (m, oi) in merge_q:
                            rcol = (m * 7 + oi) * 2
                            c.op("dve", lambda e: e.scalar_tensor_tensor(
                                out=sf_[:].rearrange("p (h q) -> p h q", h=4),
                                in0=ps_[:].rearrange("p (h q) -> p h q", h=4),
                                scalar=rowb[:, rcol:rcol + 1],
                                in1=tab[:, oi, hg * 4:(hg + 1) * 4, :],
                                op0=ALU.add, op1=ALU.add),
                                 reads=[rps, rrow, rtab], writes=[rsf])
                        for qrl in ([] if (merge_q is not None and (m, oi) in merge_q) else range(2)):
                            rcol = (m * 7 + oi) * 2 + qrl
                            c.op("dve", lambda e: e.scalar_tensor_tensor(
                                out=sf_[:].rearrange("p (h q) -> p h q", h=4)[:, :, qrl * 64:(qrl + 1) * 64],
                                in0=ps_[:].rearrange("p (h q) -> p h q", h=4)[:, :, qrl * 64:(qrl + 1) * 64],
                                scalar=rowb[:, rcol:rcol + 1],
                                in1=tab[:, oi, hg * 4:(hg + 1) * 4, qrl * 64:(qrl + 1) * 64],
                                op0=ALU.add, op1=ALU.add),
                                 reads=[rps, rrow, rtab], writes=[rsf])
                        c.op("act", lambda e: e.activation(out=pt_[:, oi, :], in_=sf_[:], func=AF.Exp), reads=[rsf], writes=[rpt])

                def st_pv(u):
                    m, hg = units[u]
                    pt_, rpt = pt[u % 2]
                    po_, rpo = pO[u % 2]
                    valid = valid_of(m)
                    for hh in range(4):
                        h = HGRP[hg][hh]
                        for vi, (oi, jc) in enumerate(valid):
                            v_, rv = vr[jc % NR]
                            c.op("pe", lambda e: e.matmul(po_[:, hh * 65:(hh + 1) * 65], lhsT=pt_[:, oi, hh * 128:(hh + 1) * 128],
                                                          rhs=v_[:, h * 65:(h + 1) * 65], start=(vi == 0), stop=(vi == len(valid) - 1)),
                                 reads=[rpt, rv], writes=[rpo], signal=(hh == 3 and vi == len(valid) - 1))

                def st_nrm(u):
                    m, hg = units[u]
                    po_, rpo = pO[u % 2]
                    o_, ro = osb[m % 2]
                    rs_, rrs = rsum[u % 2]
                    pov = po_[:, 0:260].rearrange("p (h d) -> p h d", d=65)
                    c.op("dve", lambda e: e.tensor_scalar(out=rs_[:], in0=pov[:, :, 64], scalar1=1e-20, scalar2=None, op0=ALU.max),
                         reads=[rpo], writes=[rrs])
                    c.op("dve", lambda e: e.reciprocal(rs_[:], rs_[:]), reads=[rrs], writes=[rrs])
                    for hh in range(4):
                        h = HGRP[hg][hh]
                        c.op("act", lambda e: e.activation(out=o_[:, h * 64:(h + 1) * 64], in_=po_[:, hh * 65:hh * 65 + 64], func=AF.Identity,
                                                           scale=rs_[:, hh:hh + 1]),
                             reads=[rpo, rrs], writes=[ro])

                def st_tr(u):
                    m, hg = units[u]
                    if hg != 3:
                        return
                    o_, ro = osb[m % 2]
                    p_, rp = pT[0]
                    for k in range(8):
                        c.op("pe", lambda e: e.transpose(out=p_[:, k * 128:(k + 1) * 128], in_=o_[:, k * 128:(k + 1) * 128], identity=ident[:]),
                             reads=[ro, r_ident], writes=[rp], signal=(k == 7))

                def st_ev(u):
                    m, hg = units[u]
                    if hg != 3:
                        return
                    p_, rp = pT[0]
                    ot_, rot = ot[m % 2]
                    c.op("act", lambda e: e.activation(out=ot_[:].rearrange("p k t -> p (k t)"), in_=p_[:], func=AF.Copy), reads=[rp], writes=[rot])

                def st_wo(u):
                    m, hg = units[u]
                    if hg != 3:
                        return
                    ot_, rot = ot[m % 2]
                    py_, rpy = pY[0]
                    for half in range(2):
                        for k in range(8):
                            c.op("pe", lambda e: e.matmul(py_[:, half * 512:(half + 1) * 512], lhsT=ot_[:, k, :], rhs=Wo[:, k, half * 512:(half + 1) * 512],
                                                          start=(k == 0), stop=False),
                                 reads=[rot, rW], writes=[rpy], signal=False)
                        c.op("pe", lambda e: e.matmul(py_[:, half * 512:(half + 1) * 512], lhsT=ones1[:, :], rhs=bo[:, half * 512:(half + 1) * 512],
                                                      start=False, stop=True),
                             reads=[ro1, rbo], writes=[rpy], signal=(half == 1))

                def st_res(u):
                    m, hg = units[u]
                    if hg != 3:
                        return
                    x_, rx = xs[m % 4]
                    py_, rpy = pY[0]
                    xo_, rxo = xo[m % 3]
                    c.op("dve", lambda e: e.tensor_tensor(out=xo_[:], in0=py_[:], in1=x_[:], op=ALU.add), reads=[rpy, rx], writes=[rxo])
                    c.dma("pool", "a_st%d" % (m % 3), x_dst[m * 128:(m + 1) * 128, :], xo_[:], reads=[rxo])

                def st_n1(u):
                    m, hg = units[u]
                    if hg != 3:
                        return
                    xo_, rxo = xo[m % 3]
                    s_, rss = nss[m % 3]
                    q_, rrs = nrs[m % 3]
                    j_, rj = njunk[0]
                    c.op("pool", lambda e: e.memset(s_[:], 0.0), writes=[rss])
                    c.op("act", lambda e: e.activation(out=j_[:], in_=xo_[:], func=AF.Square, scale=1.0 / 32, accum_out=s_[:]),
                         reads=[rxo], writes=[rj, rss])
                    c.op("dve", lambda e: e.tensor_scalar(out=s_[:], in0=s_[:], scalar1=1e-6, scalar2=None, op0=ALU.add), reads=[rss], writes=[rss])
                    c.op("pool", lambda e: e.tensor_tensor(out=q_[:], in0=s_[:], in1=mhalf[:, 0:1], op=ALU.pow), reads=[rss, r_mhalf], writes=[rrs])
                    c.op("pool", lambda e: e.tensor_tensor(out=q_[:], in0=q_[:], in1=tokmask[:, m:m + 1], op=ALU.mult),
                         reads=[rrs, r_tokmask], writes=[rrs])

                def st_n2(u):
                    m, hg = units[u]
                    if hg != 3:
                        return
                    xo_, rxo = xo[m % 3]
                    q_, rrs = nrs[m % 3]
                    h_, rh = nh[m % 2]
                    c.op("dve", lambda e: e.scalar_tensor_tensor(out=h_[:], in0=xo_[:], scalar=q_[:], in1=gbc[:], op0=ALU.mult, op1=ALU.mult),
                         reads=[rxo, rrs, r_g], writes=[rh])

                def st_n3(u):
                    m, hg = units[u]
                    if hg != 3:
                        return
                    h_, rh = nh[m % 2]
                    p_, rp = pT[0]
                    for k in range(8):
                        c.op("pe", lambda e: e.transpose(out=p_[:, k * 128:(k + 1) * 128], in_=h_[:, k * 128:(k + 1) * 128], identity=ident[:]),
                             reads=[rh, r_ident], writes=[rp], signal=(k == 7))

                def st_n4(u):
                    m, hg = units[u]
                    if hg != 3:
                        return
                    p_, rp = pT[0]
                    ht_, rht = nhts[0]
                    i = m - q_lo
                    c.op("dve", lambda e: e.tensor_copy(out=ht_[:, :, (i % 4) * 128:(i % 4 + 1) * 128], in_=p_[:].rearrange("p (k t) -> p k t", k=8)),
                         reads=[rp], writes=[rht])
                    if i % 4 == 3 or m == q_hi - 1:
                        g0 = (q_lo + (i // 4) * 4) * 128
                        n = (i % 4 + 1) * 128
                        c.dma("sp", "a_hst", hview[:, :, g0:g0 + n], ht_[:, :, 0:n], reads=[rht])

                run_pipe(len(units), [(st_s, 0), (st_pv, 1), (st_nrm, 2), (st_tr, 3), (st_ev, 4), (st_wo, 5), (st_res, 6),
                                      (st_n1, 7), (st_n2, 8), (st_n3, 9), (st_n4, 10)])
                c.barrier()

        def prep_glu(ph, j):
            if True:
                W = ph.sb([128, 8, 2 * D], BF16)
                wv = w_pw1[j].rearrange("(k p) n -> p k n", p=128)
                rW = {}
                for cb in range(4):
                    for hf in range(2):
                        r_ = Res()
                        lo_ = hf * D + cb * 256
                        c.dma("pool", "wg%d_%d" % (cb, hf), W[:, :, lo_:lo_ + 256], wv[:, :, lo_:lo_ + 256], writes=[r_])
                        rW[(hf, cb)] = r_
                b1 = ph.sb([128, 16], F32)
                rb = Res()
                c.dma("sp", "w_b", b1[:], bpw1_d[j], writes=[rb])
                pneg = ph.sb([1, T], BF16)
                rpn = Res()
                c.dma("pool", "w_c", pneg[:], padneg_d[:, :], writes=[rpn])
                ones1 = ph.sb([1, 128], BF16)
                ro1 = Res()
                c.op("pool", lambda e: e.memset(ones1[:], 1.0), writes=[ro1])
                return {'W': W, 'rW': rW, 'b1': b1, 'rb': rb, 'pneg': pneg, 'rpn': rpn, 'ones1': ones1, 'ro1': ro1}

        def run_glu(ph, w_, j, x_src, g_dram, t_lo=0, t_hi=None):
            t_hi = NT if t_hi is None else t_hi
            if True:
                nload, ncomp, nback = make_normer(ph, x_src, g_dram, False)
                W, rW, b1, rb, pneg, rpn, ones1, ro1 = (w_['W'], w_['rW'], w_['b1'], w_['rb'], w_['pneg'], w_['rpn'], w_['ones1'], w_['ro1'])
                hb = ph.sbs(2, [128, 8, 512], BF16)
                sg = ph.sbs(2, [128, 512], F32)
                gst = ph.sbs(2, [128, 8, 512], BF16)
                pa = ph.pss(3, [128, 512], F32)
                pg = ph.pss(3, [128, 512], F32)
                hview = hT_d.rearrange("(k p) t -> p k t", p=128)
                gview = gl_d.rearrange("(k p) t -> p k t", p=128)
                n = 0
                tk0 = t_lo * 128
                ntok = (t_hi - t_lo) * 128
                nblk = (ntok + 511) // 512

                def blkn(s):
                    return min(512, tk0 + ntok - (tk0 + s * 512))

                nload(tk0, blkn(0) // 128)
                ncomp(tk0, blkn(0) // 128)
                nback(blkn(0) // 128, hb[0][0], hb[0][1])
                for s in range(nblk):
                    h_, rh = hb[s % 2]
                    s0 = tk0 + s * 512
                    N = blkn(s)
                    if s + 1 < nblk:
                        nload(tk0 + (s + 1) * 512, blkn(s + 1) // 128)
                    st_, rst = gst[s % 2]
                    for cc in range(8):
                        pa_, rpa = pa[n % 3]
                        pg_, rpg = pg[n % 3]
                        sg_, rsg = sg[n % 2]
                        n += 1
                        if cc == 3 and s + 1 < nblk:
                            ncomp(tk0 + (s + 1) * 512, blkn(s + 1) // 128)
                        if cc == 6 and s + 1 < nblk:
                            nback(blkn(s + 1) // 128, hb[(s + 1) % 2][0], hb[(s + 1) % 2][1])
                        for k in range(8):
                            c.op("pe", lambda e: e.matmul(pa_[:, 0:N], lhsT=W[:, k, cc * 128:(cc + 1) * 128], rhs=h_[:, k, 0:N], start=(k == 0), stop=(k == 7)),
                                 reads=[rh, rW[(0, cc // 2)]], writes=[rpa], signal=(k == 7))
                        for k in range(8):
                            c.op("pe", lambda e: e.matmul(pg_[:, 0:N], lhsT=W[:, k, D + cc * 128:D + (cc + 1) * 128], rhs=h_[:, k, 0:N], start=(k == 0), stop=False),
                                 reads=[rh, rW[(1, cc // 2)]], writes=[rpg], signal=False)
                        c.op("pe", lambda e: e.matmul(pg_[:, 0:N], lhsT=ones1[:, :], rhs=pneg[:, s0:s0 + N], start=False, stop=True),
                             reads=[ro1, rpn], writes=[rpg])
                        c.op("act", lambda e: e.activation(out=sg_[:, 0:N], in_=pg_[:, 0:N], func=AF.Sigmoid, bias=b1[:, 8 + cc:9 + cc]),
                             reads=[rpg, rb], writes=[rsg])
                        c.op("dve", lambda e: e.scalar_tensor_tensor(out=st_[:, cc, 0:N], in0=pa_[:, 0:N], scalar=b1[:, cc:cc + 1], in1=sg_[:, 0:N], op0=ALU.add, op1=ALU.mult),
                             reads=[rpa, rb, rsg], writes=[rst])
                    c.dma("sp", "g_st%d" % (s % 2), gview[:, :, s0:s0 + N], st_[:, :, 0:N], reads=[rst])
                c.barrier()

        def prep_conv(ph, j):
            if True:
                W = ph.sb([128, 8, D], BF16)
                rW = Res()
                c.dma("pool", "w_a", W[:], w_pw2[j].rearrange("(k p) n -> p k n", p=128), writes=[rW])
                wdw = ph.sb([128, 8 * 31], F32)
                rwd = Res()
                c.dma("sp", "w_b", wdw[:], wdw_d[j], writes=[rwd])
                bdw = ph.sb([128, 8], F32)
                lng = ph.sb([128, 8], F32)
                lnb = ph.sb([128, 8], F32)
                rsm = Res()
                c.dma("sp", "w_c", bdw[:], bdw_d[j], writes=[rsm])
                c.dma("sp", "w_d", lng[:], lng_d[j], writes=[rsm])
                c.dma("sp", "w_e", lnb[:], lnb_d[j], writes=[rsm])
                b2 = ph.sb([1, D], BF16)
                rb2 = Res()
                c.dma("pool", "w_f", b2[:], bpw2_d[j][0:1, :], writes=[rb2])
                ones1 = ph.sb([1, 128], BF16)
                ro1 = Res()
                c.op("pool", lambda e: e.memset(ones1[:], 1.0), writes=[ro1])
                diag = ph.sb([128, 8 * 31, 128], BF16)
                rdg = Res()
                for i in range(8 * 31):
                    c.op("dve", lambda e: e.tensor_scalar(out=diag[:, i, :], in0=ident[:], scalar1=wdw[:, i:i + 1], scalar2=None, op0=ALU.mult),
                         reads=[r_ident, rwd], writes=[rdg])
                omean = ph.sb([128, 128], BF16)
                rom = Res()
                c.op("pool", lambda e: e.memset(omean[:], 1.0 / 1024), writes=[rom])
                return {'W': W, 'rW': rW, 'wdw': wdw, 'rwd': rwd, 'bdw': bdw, 'lng': lng, 'lnb': lnb, 'rsm': rsm, 'b2': b2, 'rb2': rb2, 'ones1': ones1, 'ro1': ro1, 'diag': diag, 'rdg': rdg, 'omean': omean, 'rom': rom}

        def run_conv(ph, w_, j, x_src, x_dst, t_lo=0, t_hi=None):
            t_hi = NT if t_hi is None else t_hi
            if True:
                W, rW, wdw, rwd, bdw, lng, lnb, rsm, b2, rb2, ones1, ro1, diag, rdg, omean, rom = (w_['W'], w_['rW'], w_['wdw'], w_['rwd'], w_['bdw'], w_['lng'], w_['lnb'], w_['rsm'], w_['b2'], w_['rb2'], w_['ones1'], w_['ro1'], w_['diag'], w_['rdg'], w_['omean'], w_['rom'])
                gb = ph.sbs(3, [128, 8, 542], BF16)
                cf = [(ph.sb([128, 8, 512], F32), [Res() for _ in range(8)]) for _ in range(2)]
                cb = ph.sbs(1, [128, 8, 512], BF16)
                cq = ph.sbs(1, [128, 8, 512], BF16)
                mean = ph.sbs(2, [128, 512], F32)
                rstd = ph.sbs(2, [128, 512], F32)
                zt = [(ph.sb([128, 8, 512], BF16), [Res() for _ in range(8)]) for _ in range(2)]
                xs = ph.sbs(2, [128, D], F32)
                xo = ph.sbs(2, [128, D], F32)
                pc = ph.pss(2, [128, 512], F32)
                pm = ph.pss(1, [128, 512], F32)
                pq = ph.pss(1, [128, 512], F32)
                pY = ph.pss(2, [128, D], F32)
                gview = gl_d.rearrange("(k p) t -> p k t", p=128)
                tk0 = t_lo * 128
                ntok = (t_hi - t_lo) * 128
                nb = (ntok + 511) // 512

                def blk_n(s):
                    return min(512, tk0 + ntok - (tk0 + s * 512))

                def st_l(s):
                    g_, rg = gb[s % 3]
                    N = blk_n(s)
                    lo = tk0 + s * 512 - 15
                    hi = tk0 + s * 512 + N + 15
                    clo = max(lo, 0)
                    chi = min(hi, T)
                    if clo > lo:
                        c.op("pool", lambda e: e.memset(g_[:, :, 0:clo - lo], 0.0), writes=[rg])
                    if chi < hi:
                        c.op("pool", lambda e: e.memset(g_[:, :, chi - lo:N + 30], 0.0), writes=[rg])
                    c.dma("sp", "c_g%d" % (s % 3), g_[:, :, clo - lo:chi - lo], gview[:, :, clo:chi], writes=[rg])

                def st_a(s):
                    g_, rg = gb[s % 3]
                    N = blk_n(s)
                    cf_, rcfs = cf[s % 2]
                    cb_, rcb = cb[0]
                    cq_, rcq = cq[0]
                    for cc in range(8):
                        p_, rp = pc[cc % 2]
                        for k in range(31):
                            c.op("pe", lambda e: e.matmul(p_[:, 0:N], lhsT=diag[:, cc * 31 + k, :], rhs=g_[:, cc, k:k + N], start=(k == 0), stop=(k == 30)),
                                 reads=[rg, rdg], writes=[rp], signal=(k == 30))
                        c.op("act", lambda e: e.activation(out=cf_[:, cc, 0:N], in_=p_[:, 0:N], func=AF.Identity, bias=bdw[:, cc:cc + 1]),
                             reads=[rp, rsm], writes=[rcfs[cc]])
                        c.op("act", lambda e: e.activation(out=cq_[:, cc, 0:N], in_=p_[:, 0:N], func=AF.Square, bias=bdw[:, cc:cc + 1]),
                             reads=[rp, rsm], writes=[rcq])
                        c.op("pool", lambda e: e.tensor_copy(out=cb_[:, cc, 0:N], in_=cf_[:, cc, 0:N]), reads=[rcfs[cc]], writes=[rcb])
                    pm_, rpm = pm[0]
                    pq_, rpq = pq[0]
                    for cc in range(8):
                        c.op("pe", lambda e: e.matmul(pq_[:, 0:N], lhsT=omean[:], rhs=cq_[:, cc, 0:N], start=(cc == 0), stop=(cc == 7)),
                             reads=[rcq, rom], writes=[rpq], signal=(cc == 7))
                    for cc in range(8):
                        c.op("pe", lambda e: e.matmul(pm_[:, 0:N], lhsT=omean[:], rhs=cb_[:, cc, 0:N], start=(cc == 0), stop=(cc == 7)),
                             reads=[rcb, rom], writes=[rpm], signal=(cc == 7))
                    mn_, rmn = mean[s % 2]
                    rs_, rrs = rstd[s % 2]
                    c.op("dve", lambda e: e.tensor_copy(out=mn_[:, 0:N], in_=pm_[:, 0:N]), reads=[rpm], writes=[rmn])
                    c.op("dve", lambda e: e.tensor_tensor(out=rs_[:, 0:N], in0=mn_[:, 0:N], in1=mn_[:, 0:N], op=ALU.mult), reads=[rmn], writes=[rrs])
                    c.op("dve", lambda e: e.tensor_tensor(out=rs_[:, 0:N], in0=pq_[:, 0:N], in1=rs_[:, 0:N], op=ALU.subtract), reads=[rpq, rrs], writes=[rrs])
                    c.op("dve", lambda e: e.tensor_scalar(out=rs_[:, 0:N], in0=rs_[:, 0:N], scalar1=0.0, scalar2=1e-5, op0=ALU.max, op1=ALU.add),
                         reads=[rrs], writes=[rrs])
                    c.op("act", lambda e: e.activation(out=rs_[:, 0:N], in_=rs_[:, 0:N], func=AF.Sqrt), reads=[rrs], writes=[rrs])
                    c.op("dve", lambda e: e.reciprocal(rs_[:, 0:N], rs_[:, 0:N]), reads=[rrs], writes=[rrs])

                def st_b(s):
                    N = blk_n(s)
                    cf_, rcfs = cf[s % 2]
                    mn_, rmn = mean[s % 2]
                    rs_, rrs = rstd[s % 2]
                    zt_, rzts = zt[s % 2]
                    for cc in range(8):
                        c.op("dve", lambda e: e.tensor_tensor(out=cf_[:, cc, 0:N], in0=cf_[:, cc, 0:N], in1=mn_[:, 0:N], op=ALU.subtract),
                             reads=[rcfs[cc], rmn], writes=[rcfs[cc]])
                        c.op("dve", lambda e: e.tensor_tensor(out=cf_[:, cc, 0:N], in0=cf_[:, cc, 0:N], in1=rs_[:, 0:N], op=ALU.mult),
                             reads=[rcfs[cc], rrs], writes=[rcfs[cc]])
                        c.op("act", lambda e: e.activation(out=zt_[:, cc, 0:N], in_=cf_[:, cc, 0:N], func=AF.Silu, scale=lng[:, cc:cc + 1], bias=lnb[:, cc:cc + 1]),
                             reads=[rcfs[cc], rsm], writes=[rzts[cc]])

                def st_c(s):
                    zt_, rzts = zt[s % 2]
                    for tl in range(blk_n(s) // 128):
                        nx = s * 4 + tl
                        tok0 = tk0 + s * 512 + tl * 128
                        x_, rx = xs[nx % 2]
                        xo_, rxo = xo[nx % 2]
                        py_, rpy = pY[nx % 2]
                        c.dma("sp", "c_x%d" % (nx % 2), x_[:], x_src[tok0:tok0 + 128, :], writes=[rx])
                        for half in range(2):
                            for k in range(8):
                                c.op("pe", lambda e: e.matmul(py_[:, half * 512:(half + 1) * 512], lhsT=zt_[:, k, tl * 128:(tl + 1) * 128],
                                                              rhs=W[:, k, half * 512:(half + 1) * 512], start=(k == 0), stop=False),
                                     reads=[rzts[k], rW], writes=[rpy], signal=False)
                            c.op("pe", lambda e: e.matmul(py_[:, half * 512:(half + 1) * 512], lhsT=ones1[:, :], rhs=b2[:, half * 512:(half + 1) * 512],
                                                          start=False, stop=True),
                                 reads=[ro1, rb2], writes=[rpy], signal=(half == 1))
                        c.op("dve", lambda e: e.tensor_tensor(out=xo_[:], in0=py_[:], in1=x_[:], op=ALU.add), reads=[rpy, rx], writes=[rxo])
                        c.dma("pool", "c_st%d" % (nx % 2), x_dst[tok0:tok0 + 128, :], xo_[:], reads=[rxo])

                run_pipe(nb, [(st_l, 0), (st_a, 1), (st_c, 3), (st_b, 2)])
                c.barrier()

        def prep_ffn(ph, i):
            if True:
                Wu = ph.sb([128, 8, 2 * DFF], BF16)
                wv = w_up[i].rearrange("(k p) n -> p k n", p=128)
                cblocks = [(0, 2), (2, 6), (6, 14), (14, 22)]
                rWu = {}
                for (c0, c1) in cblocks:
                    for hf in range(2):
                        r_ = Res()
                        lo_, hi_ = hf * DFF + c0 * 128, hf * DFF + c1 * 128
                        c.dma("pool", "wu%d_%d" % (c0, hf), Wu[:, :, lo_:hi_], wv[:, :, lo_:hi_], writes=[r_])
                        for ch in range(c0, c1):
                            rWu[hf * 22 + ch] = r_
                Wd = ph.sb([128, 22, D], BF16)
                rWd = {}
                wdv = w_down[i].rearrange("(k p) n -> p k n", p=128)
                for k0 in range(0, 22, 2):
                    r_ = Res()
                    c.dma("pool", "wd%d" % k0, Wd[:, k0:k0 + 2, :], wdv[:, k0:k0 + 2, :], writes=[r_])
                    rWd[k0] = r_
                    rWd[k0 + 1] = r_
                fdw = ph.sb([128, 44 * 3], F32)
                fdb = ph.sb([128, 44], F32)
                rf = Res()
                c.dma("sp", "w_c", fdw[:], fdw_d[i], writes=[rf])
                c.dma("sp", "w_d", fdb[:], fdb_d[i], writes=[rf])
                return {'Wu': Wu, 'rWu': rWu, 'Wd': Wd, 'rWd': rWd, 'fdw': fdw, 'fdb': fdb, 'rf': rf}

        def run_ffn(ph, w_, i, x_src, x_dst, tok_lo=0, tok_hi=None, fuse_final=False):
            tok_hi = T if tok_hi is None else tok_hi
            if True:
                Wu, rWu, Wd, rWd, fdw, fdb, rf = (w_['Wu'], w_['rWu'], w_['Wd'], w_['rWd'], w_['fdw'], w_['fdb'], w_['rf'])
                hb = ph.sbs(2, [128, 8, 512], BF16)
                tt = ph.sbs(4, [128, 512], F32)
                sgl = ph.sbs(2, [128, 512], F32)
                yT = ph.sbs(1, [128, 22, 512], BF16)
                xs = ph.sbs(2, [128, D], F32)
                xo = ph.sbs(2, [128, D], F32)
                pU = ph.pss(4, [128, 512], F32)
                pY = ph.pss(2, [128, D], F32)
                if fuse_final:
                    gfin = ph.sb([128, D], F32)
                    rgf = Res()
                    c.dma("sp", "w_gf", gfin[:], gfin_d[:, :], writes=[rgf])
                    fss = ph.sbs(2, [128, 1], F32)
                    frs = ph.sbs(2, [128, 1], F32)
                hview = hT_d.rearrange("(k p) t -> p k t", p=128)
                nblk = (tok_hi - tok_lo + FB - 1) // FB
                nu = 0
                nx = 0
                y_, ry = yT[0]
                def load_h(b):
                    s0 = tok_lo + b * FB
                    L = min(FB, tok_hi - s0)
                    h_, rh = hb[b % 2]
                    lo = s0 - 1
                    hi = s0 + L + 1
                    clo = max(lo, 0)
                    chi = min(hi, T)
                    if clo > lo:
                        c.op("pool", lambda e: e.memset(h_[:, :, 0:1], 0.0), writes=[rh])
                    if chi < hi:
                        c.op("pool", lambda e: e.memset(h_[:, :, L + 1:L + 2], 0.0), writes=[rh])
                    c.dma("sp", "f_h%d" % (b % 2), h_[:, :, clo - lo:chi - lo], hview[:, :, clo:chi], writes=[rh])

                load_h(0)
                for b in range(nblk):
                    s0 = tok_lo + b * FB
                    L = min(FB, tok_hi - s0)
                    h_, rh = hb[b % 2]
                    if b + 1 < nblk:
                        load_h(b + 1)
                    for pi in range(22):
                        tg = None
                        for which in range(2):
                            ch = which * 22 + pi
                            p_, rp = pU[nu % 4]
                            t_, rt = tt[nu % 4]
                            nu += 1
                            for k in range(8):
                                c.op("pe", lambda e: e.matmul(p_[:, 0:L + 2], lhsT=Wu[:, k, ch * 128:(ch + 1) * 128], rhs=h_[:, k, 0:L + 2],
                                                              start=(k == 0), stop=(k == 7)),
                                     reads=[rh, rWu[ch]], writes=[rp], signal=(k == 7))
                            c.op("act", lambda e: e.activation(out=t_[:, 0:L], in_=p_[:, 1:L + 1], func=AF.Identity,
                                                               scale=fdw[:, ch * 3 + 1:ch * 3 + 2], bias=fdb[:, ch:ch + 1]),
                                 reads=[rp, rf], writes=[rt])
                            c.op("dve", lambda e: e.scalar_tensor_tensor(out=t_[:, 0:L], in0=p_[:, 0:L], scalar=fdw[:, ch * 3:ch * 3 + 1], in1=t_[:, 0:L],
                                                                         op0=ALU.mult, op1=ALU.add),
                                 reads=[rp, rf, rt], writes=[rt])
                            c.op("dve", lambda e: e.scalar_tensor_tensor(out=t_[:, 0:L], in0=p_[:, 2:L + 2], scalar=fdw[:, ch * 3 + 2:ch * 3 + 3], in1=t_[:, 0:L],
                                                                         op0=ALU.mult, op1=ALU.add),
                                 reads=[rp, rf, rt], writes=[rt])
                            if which == 0:
                                tg = (t_, rt)
                            else:
                                sg_, rsg = sgl[pi % 2]
                                c.op("act", lambda e: e.activation(out=sg_[:, 0:L], in_=tg[0][:, 0:L], func=AF.Silu), reads=[tg[1]], writes=[rsg])
                                c.op("dve", lambda e: e.tensor_tensor(out=y_[:, pi, 0:L], in0=sg_[:, 0:L], in1=t_[:, 0:L], op=ALU.mult),
                                     reads=[rsg, rt], writes=[ry])
                    ntl = (L + 127) // 128
                    for tl in range(ntl):
                        M = min(128, L - tl * 128)
                        tok0 = s0 + tl * 128
                        x_, rx = xs[nx % 2]
                        xo_, rxo = xo[nx % 2]
                        py_, rpy = pY[nx % 2]
                        c.dma("sp", "f_x%d" % (nx % 2), x_[0:M, :], x_src[tok0:tok0 + M, :], writes=[rx])
                        for half in range(2):
                            for k in range(22):
                                c.op("pe", lambda e: e.matmul(py_[0:M, half * 512:(half + 1) * 512], lhsT=y_[:, k, tl * 128:tl * 128 + M],
                                                              rhs=Wd[:, k, half * 512:(half + 1) * 512], start=(k == 0), stop=(k == 21)),
                                     reads=[ry, rWd[k]], writes=[rpy], signal=(half == 1 and k == 21))
                        c.op("dve", lambda e: e.tensor_tensor(out=xo_[0:M, :], in0=py_[0:M, :], in1=x_[0:M, :], op=ALU.add), reads=[rpy, rx], writes=[rxo])
                        if not fuse_final:
                            c.dma("pool", "f_st%d" % (nx % 2), x_dst[tok0:tok0 + M, :], xo_[0:M, :], reads=[rxo])
                        else:
                            s_, rss = fss[nx % 2]
                            q_, rrs = frs[nx % 2]
                            c.op("pool", lambda e: e.memset(s_[0:M, :], 0.0), writes=[rss])
                            c.op("act", lambda e: e.activation(out=x_[0:M, :], in_=xo_[0:M, :], func=AF.Square, scale=1.0 / 32, accum_out=s_[0:M, :]),
                                 reads=[rxo], writes=[rx, rss])
                            c.op("dve", lambda e: e.tensor_scalar(out=s_[0:M, :], in0=s_[0:M, :], scalar1=1e-6, scalar2=None, op0=ALU.add),
                                 reads=[rss], writes=[rss])
                            c.op("pool", lambda e: e.tensor_tensor(out=q_[0:M, :], in0=s_[0:M, :], in1=mhalf[0:M, 0:1], op=ALU.pow),
                                 reads=[rss, r_mhalf], writes=[rrs])
                            c.op("dve", lambda e: e.scalar_tensor_tensor(out=xo_[0:M, :], in0=xo_[0:M, :], scalar=q_[0:M, :], in1=gfin[0:M, :],
                                                                         op0=ALU.mult, op1=ALU.mult),
                                 reads=[rxo, rrs, rgf], writes=[rxo])
                            yr0 = tok0 - out_tile0 * 128
                            c.dma("pool", "f_st%d" % (nx % 2), y_out[yr0:yr0 + M, :], xo_[0:M, :], reads=[rxo])
                        nx += 1
                c.barrier()

        def phase_final(x_src):
            with Phase(nc, "fin") as ph:
                gbc = ph.sb([128, D], F32)
                r_g = Res()
                c.dma("sp", "w_a", gbc[:], gfin_d[:, :], writes=[r_g])
                NS = 8
                xs = ph.sbs(NS, [128, D], F32)
                junk = ph.sbs(2, [128, D], BF16)
                ss = ph.sbs(NS, [128, 1], F32)
                rs = ph.sbs(NS, [128, 1], F32)
                ys = ph.sbs(3, [128, D], F32)
                t0 = out_tile0

                def st_l(ti):
                    t = t0 + ti
                    x_, rx = xs[ti % NS]
                    c.dma("sp", "n_x%d" % (ti % NS), x_[:], x_src[t * 128:(t + 1) * 128, :], writes=[rx])

                def st_a(ti):
                    x_, rx = xs[ti % NS]
                    j_, rj = junk[ti % 2]
                    s_, rss = ss[ti % NS]
                    q_, rrs = rs[ti % NS]
                    c.op("pool", lambda e: e.memset(s_[:], 0.0), writes=[rss])
                    c.op("act", lambda e: e.activation(out=j_[:], in_=x_[:], func=AF.Square, scale=1.0 / 32, accum_out=s_[:]),
                         reads=[rx], writes=[rj, rss])
                    c.op("dve", lambda e: e.tensor_scalar(out=s_[:], in0=s_[:], scalar1=1e-6, scalar2=None, op0=ALU.add),
                         reads=[rss], writes=[rss])
                    c.op("pool", lambda e: e.tensor_tensor(out=q_[:], in0=s_[:], in1=mhalf[:, 0:1], op=ALU.pow),
                         reads=[rss, r_mhalf], writes=[rrs])

                def st_b(ti):
                    x_, rx = xs[ti % NS]
                    q_, rrs = rs[ti % NS]
                    y_, ry = ys[ti % 3]
                    c.op("dve", lambda e: e.scalar_tensor_tensor(out=y_[:], in0=x_[:], scalar=q_[:], in1=gbc[:], op0=ALU.mult, op1=ALU.mult),
                         reads=[rx, rrs, r_g], writes=[ry])
                    c.dma("pool", "fin_st%d" % (ti % 3), y_out[ti * 128:(ti + 1) * 128, :], y_[:], reads=[ry])

                run_pipe(n_final_tiles, [(st_l, 0), (st_a, 3), (st_b, 5)])
                c.barrier()

        def phase_dump(x_src):
            with Phase(nc, "dump") as ph:
                xs = ph.sbs(2, [128, D], F32)
                for t in range(NT):
                    x_, rx = xs[t % 2]
                    c.dma("sp", "d_x%d" % (t % 2), x_[:], x_src[t * 128:(t + 1) * 128, :], writes=[rx])
                    c.dma("sp", "d_s%d" % (t % 2), y_out[t * 128:(t + 1) * 128, :], x_[:], reads=[rx])
                c.barrier()

        state = {"cur": xw, "nxt": xa}

        def swap():
            cur, nxt = state["cur"], state["nxt"]
            state["cur"], state["nxt"] = nxt, (xb if nxt is xa else xa)

        steps = []
        full = (wr == WR and nphases is None)
        RNG = {"attn2": (3, 58), "nf2": (3, 58), "ffn2": (4, 57), "nm3": (4, 57), "glu3": (4, 57), "conv3": (5, 56), "nf3": (5, 56), "ffn3": (6, 55)}

        if dbg_rng is not None:
            RNG, full = dbg_rng, True

        def rng(key):
            return RNG[key] if full and key in RNG else (0, NT)

        for i in range(DEPTH):
            j = i // 2
            if i % 2 == 0:
                steps.append(("qkv", lambda ph, j=j: prep_qkv(ph, j), lambda ph, w, j=j, i=i: run_qkv(ph, w, j, state["cur"], gmix_d[i])))
                steps.append(("attn", lambda ph, j=j: prep_attn(ph, j), lambda ph, w, j=j, i=i: (run_attn(ph, w, j, state["cur"], state["nxt"], gffn_d[i], *rng("attn%d" % i)), swap())))
            else:
                steps.append(("glu", lambda ph, j=j: prep_glu(ph, j), lambda ph, w, j=j, i=i: run_glu(ph, w, j, state["cur"], gmix_d[i], *rng("glu%d" % i))))
                steps.append(("conv", lambda ph, j=j: prep_conv(ph, j), lambda ph, w, j=j, i=i: (run_conv(ph, w, j, state["cur"], state["nxt"], *rng("conv%d" % i)), swap())))
            if i % 2 == 1:
                steps.append(("norm", None, lambda ph, w, i=i: phase_norm(ph, state["cur"], gffn_d[i], True, "f%d" % i, *rng("nf%d" % i))))
            steps.append(("ffn", lambda ph, i=i: prep_ffn(ph, i), lambda ph, w, i=i: (run_ffn(ph, w, i, state["cur"], state["nxt"], rng("ffn%d" % i)[0] * 128, rng("ffn%d" % i)[1] * 128,
                                                       fuse_final=(nphases is None and i == DEPTH - 1)), swap())))
        if nphases is not None:
            steps = steps[:nphases]
        can_prefetch_during = set()
        phs = [None] * len(steps)
        ws = [None] * len(steps)

        def do_prep(k):
            name, prep, run = steps[k]
            phs[k] = Phase(nc, "%s%d" % (name, k), side=("left" if k % 2 == 0 else "right"))
            phs[k].__enter__()
            ws[k] = prep(phs[k]) if prep is not None else None

        for k in range(len(steps)):
            if phs[k] is None:
                do_prep(k)
            if k + 1 < len(steps) and steps[k][0] in can_prefetch_during and steps[k + 1][1] is not None:
                do_prep(k + 1)
            steps[k][2](phs[k], ws[k])
            phs[k].__exit__(None, None, None)
        if nphases is not None:
            phase_dump(state["cur"])
        c.barrier()
        c.close()
    return nc


def _row_info():
    seq = -np.ones(GROWS, np.int64)
    rr = -np.ones(GROWS, np.int64)
    for s, off in enumerate(SEQ_OFF):
        seq[off:off + ROWS] = s
        rr[off:off + ROWS] = np.arange(ROWS)
    return seq, rr


def _core_tables(core, seq, rr, g0=None, WR=WR, ROWS=ROWS):
    GROWS = seq.shape[0]
    NT = WR // 2
    T = WR * GW
    if g0 is None:
        g0 = core * CORE_ROWS - HALO_B
    grow = g0 + np.arange(WR)
    inr = (grow >= 0) & (grow < GROWS)
    gs = np.where(inr, seq[np.clip(grow, 0, GROWS - 1)], -1)
    gr = np.where(inr, rr[np.clip(grow, 0, GROWS - 1)], -1)
    valid_row = gs >= 0
    tokvalid = np.repeat(valid_row, GW)
    tokmask = tokvalid.reshape(NT, 128).T.astype(np.float32).copy()
    padneg = np.where(tokvalid, 0.0, NEG).astype(np.float32)[None, :]
    rb = np.zeros((128, NT, 7, 2), np.float32)
    for m in range(NT):
        for qrl in range(2):
            qrow = 2 * m + qrl
            for oi, o in enumerate(OFFS):
                for krl in range(2):
                    krow = 2 * (m + o) + krl
                    ok = True
                    if valid_row[qrow]:
                        if krow < 0 or krow >= WR or gs[krow] != gs[qrow]:
                            ok = False
                        else:
                            r = gr[qrow]
                            rs = min(max(r - 4, 0), ROWS - 8)
                            ok = rs <= gr[krow] < rs + 8
                    else:
                        ok = (krow == qrow)
                    if not ok:
                        rb[krl * 64:(krl + 1) * 64, m, oi, qrl] = NEG
    return grow, inr, gs, gr, tokmask, padneg, rb.reshape(128, NT * 14)


def _needed_offsets():
    seq, rr = _row_info()
    need = np.zeros((NT, 7), bool)
    for core in range(NCORE):
        rb = _core_tables(core, seq, rr)[6].reshape(128, NT, 7, 2)
        need |= (rb == 0).any(axis=(0, 3))
    return [set(np.nonzero(need[m])[0].tolist()) for m in range(NT)]


def _mergeable_q(tables=None):
    if tables is None:
        seq, rr = _row_info()
        tables = [_core_tables(core, seq, rr)[6] for core in range(NCORE)]
    nt = tables[0].shape[1] // 14
    same = np.ones((nt, 7), bool)
    for rb in tables:
        rb = rb.reshape(128, nt, 7, 2)
        same &= (rb[:, :, :, 0] == rb[:, :, :, 1]).all(axis=0)
    return {(m, oi) for m in range(nt) for oi in range(7) if same[m, oi]}


def _bias_table(rpb):
    kc = np.arange(GW)[:, None]
    qc = np.arange(GW)[None, :]
    ws = np.clip(qc - 8, 0, GW - 16)
    col_ok = (kc >= ws) & (kc < ws + 16)
    dc = np.clip(kc - qc + 15, 0, 30)
    tab = np.full((2, GW, 7, NH, 2, GW), NEG, np.float32)
    for oi, o in enumerate(OFFS):
        for krl in range(2):
            for qrl in range(2):
                delta = 2 * o + krl - qrl
                if delta < -7 or delta > 7:
                    continue
                blk = rpb[:, delta + 7, :][:, dc]
                blk = np.where(col_ok[None], blk, np.float32(NEG))
                tab[krl, :, oi, :, qrl, :] = blk.transpose(1, 0, 2)
    order = [h for g in HGRP for h in g]
    tab = tab[:, :, :, order]
    return tab.reshape(128, 7 * NH * 128)


def _pc(v, nchunk):
    return np.ascontiguousarray(v.reshape(nchunk, 128).T)


def _bc(v):
    return np.ascontiguousarray(np.broadcast_to(v[None, :], (128, v.shape[0])))


_NC_CACHE = {}


def kernel(x_prompt, x_sample, attn_w_qkv, attn_b_qkv, attn_rpb, attn_w_o, attn_b_o,
           conv_w_pw1, conv_b_pw1, conv_w_dw, conv_b_dw, conv_ln_g, conv_ln_b, conv_w_pw2, conv_b_pw2,
           ffn_w_up, ffn_w_dw, ffn_b_dw, ffn_w_down, norm_mix, norm_ffn, norm_final):
    f = lambda a: np.asarray(a, dtype=np.float32)
    x_prompt, x_sample = f(x_prompt), f(x_sample)
    seq, rr = _row_info()
    xg = np.zeros((GROWS, GW, D), np.float32)
    seqs = [x_prompt[0], x_prompt[1], x_sample[0]]
    for s, off in enumerate(SEQ_OFF):
        xg[off:off + ROWS] = seqs[s].reshape(ROWS, GW, D)

    shared = {
        "ident": np.eye(128, dtype=np.float32),
        "w_qkv": f(attn_w_qkv),
        "bqk": np.stack([_pc(f(attn_b_qkv)[j, :2 * D], 16) for j in range(2)]),
        "bv_bc": np.stack([_bc(f(attn_b_qkv)[j, 2 * D:]) for j in range(2)]),
        "tab": np.stack([_bias_table(f(attn_rpb)[j]) for j in range(2)]),
        "w_o": f(attn_w_o),
        "bo_bc": np.stack([_bc(f(attn_b_o)[j]) for j in range(2)]),
        "w_pw1": f(conv_w_pw1),
        "bpw1": np.stack([_pc(f(conv_b_pw1)[j], 16) for j in range(2)]),
        "wdw": np.stack([np.ascontiguousarray(f(conv_w_dw)[j].T.reshape(8, 128, 31).transpose(1, 0, 2).reshape(128, 8 * 31)) for j in range(2)]),
        "bdw": np.stack([_pc(f(conv_b_dw)[j], 8) for j in range(2)]),
        "lng": np.stack([_pc(f(conv_ln_g)[j], 8) for j in range(2)]),
        "lnb": np.stack([_pc(f(conv_ln_b)[j], 8) for j in range(2)]),
        "w_pw2": f(conv_w_pw2),
        "bpw2_bc": np.stack([_bc(f(conv_b_pw2)[j]) for j in range(2)]),
        "w_up": f(ffn_w_up),
        "fdw": np.stack([np.ascontiguousarray(f(ffn_w_dw)[i].T.reshape(44, 128, 3).transpose(1, 0, 2).reshape(128, 44 * 3)) for i in range(4)]),
        "fdb": np.stack([_pc(f(ffn_b_dw)[i], 44) for i in range(4)]),
        "w_down": f(ffn_w_down),
        "gmix_bc": np.stack([_bc(f(norm_mix)[i]) for i in range(4)]),
        "gffn_bc": np.stack([_bc(f(norm_ffn)[i]) for i in range(4)]),
        "gfin_bc": _bc(f(norm_final)),
    }
    in_maps = []
    for core in range(NCORE):
        grow, inr, gs, gr, tokmask, padneg, rowbias = _core_tables(core, seq, rr)
        xwin = np.zeros((WR, GW, D), np.float32)
        xwin[inr] = xg[grow[inr]]
        m = dict(shared)
        m["xw"] = xwin.reshape(T, D)
        m["tokmask"] = tokmask
        m["padneg"] = padneg
        m["rowbias"] = rowbias
        in_maps.append(m)

    if "nc" not in _NC_CACHE:
        _NC_CACHE["nc"] = build_program(tile_offs=_needed_offsets(), merge_q=_mergeable_q())
    res = run_bass_kernel_spmd(_NC_CACHE["nc"], in_maps, core_ids=list(range(NCORE)))
    yg = np.zeros((GROWS, GW, D), np.float32)
    for core in range(NCORE):
        yg[core * CORE_ROWS:(core + 1) * CORE_ROWS] = np.asarray(res.results[core]["y"], dtype=np.float32).reshape(CORE_ROWS, GW, D)
    outs = [yg[off:off + ROWS].reshape(ROWS * GW, D) for off in SEQ_OFF]
    y_prompt = np.stack([outs[0], outs[1]]).astype(np.float32)
    y_sample = outs[2][None].astype(np.float32)
    return (y_prompt, y_sample)
```
